# Optimizing a Trainium2 kernel written in Bass

```python
import math
import jax, jax.numpy as jnp
from jax import lax
import numpy as np

D_MODEL = 2048
BATCH = 4
SEQ = 4096
DEPTH = 2

CTX_LEN = 256
GRID_W = 64
N_EVEN = (DEPTH + 1) // 2
N_ODD = DEPTH // 2
N_MOD = 9
D_FF = 5632
EPS = 1e-6

SSM_HEADS = 32
SSM_HEAD_DIM = 64
SSM_INNER = SSM_HEADS * SSM_HEAD_DIM
SSM_GROUPS = 4
SSM_STATE = 128
SSM_CONV = 5
SSM_CHUNK = 128
SSM_GN = SSM_GROUPS * SSM_STATE
SSM_CONV_CH = SSM_INNER + 2 * SSM_GN

MLA_HEADS = 16
Q_LORA = 512
KV_LORA = 512
QK_NOPE = 128
QK_ROPE = 64
V_HEAD = 128
MLA_OUT = MLA_HEADS * V_HEAD
MLA_SCALE = (QK_NOPE + QK_ROPE) ** -0.5
ROPE_THETA = 10000.0
ROPE_HALF = QK_ROPE // 2
ROPE_AXIS_FREQS = QK_ROPE // 4
Q_BLOCK = 128

IN_SIZES = (Q_LORA, KV_LORA, QK_ROPE, SSM_INNER, SSM_CONV_CH, 2 * SSM_HEADS)
IN_OFFSETS = tuple(int(v) for v in np.cumsum(IN_SIZES)[:-1])
IN_TOTAL = sum(IN_SIZES)
MIX_WIDTH = SSM_INNER + MLA_OUT

POOL_WINDOWS = (2, 4, 8, 16)
POOL_GROUP = D_MODEL // len(POOL_WINDOWS)

kernel_name = 'hybrid_ssd_mla_pool_macaron_dit'


def rmsnorm(u, gain=None):
    uf = u.astype(jnp.float32)
    y = uf * lax.rsqrt(jnp.mean(uf * uf, axis=-1, keepdims=True) + EPS)
    if gain is not None:
        y = y * gain
    return y.astype(u.dtype)


def modulate(u, shift, scale):
    return u * (1 + scale) + shift


def adaln(cond, w, b):
    return jnp.split(jax.nn.silu(cond) @ w + b, N_MOD, axis=-1)


def swiglu(u, wg, wu, wd):
    return (jax.nn.silu(u @ wg) * (u @ wu)) @ wd


def axial_rope(rows):
    f32 = jnp.float32
    row = jnp.repeat(jnp.arange(rows, dtype=f32), GRID_W)
    col = jnp.tile(jnp.arange(GRID_W, dtype=f32), rows)
    inv = ROPE_THETA ** (-jnp.arange(ROPE_AXIS_FREQS, dtype=f32) / ROPE_AXIS_FREQS)
    ang = jnp.concatenate([row[:, None] * inv, col[:, None] * inv], axis=-1)
    return jnp.cos(ang), jnp.sin(ang)


def apply_rope(u, cos, sin):
    u1, u2 = u[..., :ROPE_HALF], u[..., ROPE_HALF:]
    return jnp.concatenate([u1 * cos - u2 * sin, u1 * sin + u2 * cos], axis=-1).astype(u.dtype)


def dconv(u, w, bias):
    y = lax.conv_general_dilated(u, w[:, None, :], window_strides=(1,),
                                 padding=((SSM_CONV // 2, SSM_CONV // 2),),
                                 dimension_numbers=('NWC', 'WIO', 'NWC'),
                                 feature_group_count=u.shape[-1])
    return y + bias


def segsum(a):
    t = a.shape[-1]
    ar = jnp.broadcast_to(a[..., :, None], a.shape + (t,))
    ar = jnp.where(jnp.tril(jnp.ones((t, t), bool), -1), ar, 0.0)
    s = jnp.cumsum(ar, axis=-2)
    return jnp.where(jnp.tril(jnp.ones((t, t), bool)), s, -jnp.inf)


def ssd(x, dt, a_neg, bm, cm, init):
    bsz, n, nh, hp = x.shape
    ng, ns = bm.shape[2], bm.shape[3]
    nr = nh // ng
    nc, cl = n // SSM_CHUNK, SSM_CHUNK
    f32 = jnp.float32
    xd = (x.astype(f32) * dt[..., None]).reshape(bsz, nc, cl, ng, nr, hp)
    a = (dt * a_neg).reshape(bsz, nc, cl, ng, nr).transpose(0, 3, 4, 1, 2)
    bc = bm.astype(f32).reshape(bsz, nc, cl, ng, ns)
    cc = cm.astype(f32).reshape(bsz, nc, cl, ng, ns)
    a_cs = jnp.cumsum(a, axis=-1)
    cb = jnp.einsum('bclgn,bcsgn->bcgls', cc, bc)
    y_diag = jnp.einsum('bcgls,bgrcls,bcsgrp->bclgrp', cb, jnp.exp(segsum(a)), xd)
    decay_states = jnp.exp(a_cs[..., -1:] - a_cs)
    states = jnp.einsum('bclgn,bgrcl,bclgrp->bcgrpn', bc, decay_states, xd)
    states = jnp.concatenate([init.astype(f32).reshape(bsz, 1, ng, nr, hp, ns), states], axis=1)
    decay_chunk = jnp.exp(segsum(jnp.pad(a_cs[..., -1], ((0, 0), (0, 0), (0, 0), (1, 0)))))
    new_states = jnp.einsum('bgrzc,bcgrpn->bzgrpn', decay_chunk, states)
    y_off = jnp.einsum('bclgn,bcgrpn,bgrcl->bclgrp', cc, new_states[:, :-1], jnp.exp(a_cs))
    y = (y_diag + y_off).reshape(bsz, n, nh, hp).astype(x.dtype)
    return y, new_states[:, -1].reshape(bsz, nh, hp, ns)


def gated_group_rmsnorm(y, z, gain):
    g = (y * jax.nn.silu(z)).astype(jnp.float32)
    gg = g.reshape(g.shape[:-1] + (SSM_GROUPS, SSM_INNER // SSM_GROUPS))
    gg = gg * lax.rsqrt(jnp.mean(gg * gg, axis=-1, keepdims=True) + EPS)
    return (gg.reshape(g.shape) * gain).astype(y.dtype)


def mla_attend(q_nope, q_rope, k_nope, k_rope, v):
    bsz, n, nh, _ = q_nope.shape
    nb = n // Q_BLOCK
    qn = jnp.moveaxis(q_nope.reshape(bsz, nb, Q_BLOCK, nh, QK_NOPE), 1, 0)
    qr = jnp.moveaxis(q_rope.reshape(bsz, nb, Q_BLOCK, nh, QK_ROPE), 1, 0)

    def block(qs):
        qn_b, qr_b = qs
        s = jnp.einsum('bqhd,bkhd->bhqk', qn_b, k_nope) + jnp.einsum('bqhd,bkd->bhqk', qr_b, k_rope)
        pr = jax.nn.softmax(s.astype(jnp.float32) * MLA_SCALE, axis=-1).astype(v.dtype)
        return jnp.einsum('bhqk,bkhd->bqhd', pr, v)

    o = lax.map(block, (qn, qr))
    return jnp.moveaxis(o, 0, 1).reshape(bsz, n, nh * V_HEAD)


def ssd_mla_mixer(h, hc, ctx_out, cos, sin, w_in, conv_w, conv_b, dt_bias, a_log, d_skip,
                  ssm_norm, q_norm, kv_norm, w_uq, w_ukv, w_out):
    bsz, n, _ = h.shape
    a_neg = -jnp.exp(a_log.astype(jnp.float32))
    flip = lambda t: jnp.flip(t, axis=1)

    def mla_q(qc, rot):
        q = (rmsnorm(qc, q_norm) @ w_uq).reshape(bsz, qc.shape[1], MLA_HEADS, QK_NOPE + QK_ROPE)
        qr = q[..., QK_NOPE:]
        if rot:
            qr = apply_rope(qr, cos[:, None, :], sin[:, None, :])
        return q[..., :QK_NOPE], qr

    def mla_kv(kvc, kr, rot):
        kv = (rmsnorm(kvc, kv_norm) @ w_ukv).reshape(bsz, kvc.shape[1], MLA_HEADS, QK_NOPE + V_HEAD)
        if rot:
            kr = apply_rope(kr, cos, sin)
        return kv[..., :QK_NOPE], kr, kv[..., QK_NOPE:]

    def ssm_prep(xbc, dtr):
        m = xbc.shape[1]
        u = jax.nn.silu(dconv(xbc, conv_w, conv_b))
        xs = u[..., :SSM_INNER].reshape(bsz, m, SSM_HEADS, SSM_HEAD_DIM)
        bm = u[..., SSM_INNER:SSM_INNER + SSM_GN].reshape(bsz, m, SSM_GROUPS, SSM_STATE)
        cm = u[..., SSM_INNER + SSM_GN:].reshape(bsz, m, SSM_GROUPS, SSM_STATE)
        dt = jax.nn.softplus(dtr.astype(jnp.float32).reshape(bsz, m, 2, SSM_HEADS)
                             + dt_bias.astype(jnp.float32))
        return xs, bm, cm, dt[:, :, 0], dt[:, :, 1]

    def bidir(xs, bm, cm, dtf, dtb, init_f, init_b):
        yf, sf = ssd(xs, dtf, a_neg[0], bm, cm, init_f)
        yb, sb = ssd(flip(xs), flip(dtb), a_neg[1], flip(bm), flip(cm), init_b)
        return yf + flip(yb) + d_skip[:, None] * xs, sf, sb

    q_c, kv_c, k_r, z, xbc, dtr = jnp.split(h @ w_in, IN_OFFSETS, axis=-1)
    q_cc, kv_cc, k_rc, z_c, xbc_c, dtr_c = jnp.split(hc @ w_in, IN_OFFSETS, axis=-1)

    zeros = jnp.zeros((bsz, SSM_HEADS, SSM_HEAD_DIM, SSM_STATE), jnp.float32)
    y_c, sf_c, sb_c = bidir(*ssm_prep(xbc_c, dtr_c), zeros, zeros)
    kn_c, kr_c, v_c = mla_kv(kv_cc, k_rc, False)

    y_l, _, _ = bidir(*ssm_prep(xbc, dtr), sf_c, sb_c)
    kn_l, kr_l, v_l = mla_kv(kv_c, k_r, True)
    qn, qr = mla_q(q_c, True)
    o = mla_attend(qn, qr, jnp.concatenate([kn_c, kn_l], axis=1),
                   jnp.concatenate([kr_c, kr_l], axis=1), jnp.concatenate([v_c, v_l], axis=1))
    y_ssm = gated_group_rmsnorm(y_l.reshape(bsz, n, SSM_INNER), z, ssm_norm)
    out = jnp.concatenate([y_ssm, o], axis=-1) @ w_out
    if not ctx_out:
        return out, None
    qn_c, qr_c = mla_q(q_cc, False)
    o_c = mla_attend(qn_c, qr_c, kn_c, kr_c, v_c)
    y_ssm_c = gated_group_rmsnorm(y_c.reshape(bsz, hc.shape[1], SSM_INNER), z_c, ssm_norm)
    out_c = jnp.concatenate([y_ssm_c, o_c], axis=-1) @ w_out
    return out, out_c


def pool_mixer(h, w_pool, scale):
    n = h.shape[1]
    hf = h.astype(jnp.float32)
    cs = jnp.pad(jnp.cumsum(hf, axis=1), ((0, 0), (1, 0), (0, 0)))
    t = jnp.arange(n)
    groups = []
    for gi, w in enumerate(POOL_WINDOWS):
        lo = jnp.clip(t - w // 2, 0, n)
        hi = jnp.clip(t + w // 2, 0, n)
        sl = slice(gi * POOL_GROUP, (gi + 1) * POOL_GROUP)
        csg = cs[:, :, sl]
        mean = (csg[:, hi] - csg[:, lo]) / (hi - lo).astype(jnp.float32)[:, None]
        groups.append(mean - hf[:, :, sl])
    pooled = jnp.stack(groups, axis=2).astype(h.dtype)
    out = jnp.einsum('bngi,gio->bngo', pooled, w_pool).reshape(h.shape)
    return out * scale


def setup_inputs(seed: int = 0) -> dict:
    key = jax.random.key(seed)
    ks = iter(jax.random.split(key, 32))
    f32 = jnp.float32

    def nrm(shape, scale):
        return jax.random.normal(next(ks), shape, f32) * scale

    def gain(shape):
        return 1.0 + nrm(shape, 0.05)

    dt0 = jnp.exp(jax.random.uniform(next(ks), (N_EVEN, 2, SSM_HEADS), f32, math.log(1e-3), math.log(1e-1)))
    a0 = jax.random.uniform(next(ks), (N_EVEN, 2, SSM_HEADS), f32, 1.0, 16.0)
    return {
        'x': nrm((BATCH, SEQ, D_MODEL), 1.0),
        'c': nrm((BATCH, D_MODEL), 1.0),
        'ctx': nrm((BATCH, CTX_LEN, D_MODEL), 1.0),
        'c_ctx': nrm((D_MODEL,), 1.0),
        'mod_w': nrm((DEPTH, D_MODEL, N_MOD * D_MODEL), 0.5 * D_MODEL ** -0.5),
        'mod_b': nrm((DEPTH, N_MOD * D_MODEL), 0.01),
        'ffn_w_gate': nrm((DEPTH, 2, D_MODEL, D_FF), D_MODEL ** -0.5),
        'ffn_w_up': nrm((DEPTH, 2, D_MODEL, D_FF), D_MODEL ** -0.5),
        'ffn_w_down': nrm((DEPTH, 2, D_FF, D_MODEL), D_FF ** -0.5),
        'w_in': nrm((N_EVEN, D_MODEL, IN_TOTAL), D_MODEL ** -0.5),
        'conv_w': nrm((N_EVEN, SSM_CONV, SSM_CONV_CH), SSM_CONV ** -0.5),
        'conv_b': nrm((N_EVEN, SSM_CONV_CH), 0.01),
        'dt_bias': dt0 + jnp.log(-jnp.expm1(-dt0)),
        'a_log': jnp.log(a0),
        'd_skip': gain((N_EVEN, SSM_HEADS)),
        'ssm_norm': gain((N_EVEN, SSM_INNER)),
        'q_norm': gain((N_EVEN, Q_LORA)),
        'kv_norm': gain((N_EVEN, KV_LORA)),
        'w_uq': nrm((N_EVEN, Q_LORA, MLA_HEADS * (QK_NOPE + QK_ROPE)), Q_LORA ** -0.5),
        'w_ukv': nrm((N_EVEN, KV_LORA, MLA_HEADS * (QK_NOPE + V_HEAD)), KV_LORA ** -0.5),
        'w_out': nrm((N_EVEN, MIX_WIDTH, D_MODEL), MIX_WIDTH ** -0.5),
        'pool_w': nrm((N_ODD, len(POOL_WINDOWS), POOL_GROUP, POOL_GROUP), POOL_GROUP ** -0.5),
        'pool_scale': gain((N_ODD, D_MODEL)),
        'final_norm': gain((D_MODEL,)),
    }


def reference(x, c, ctx, c_ctx, mod_w, mod_b, ffn_w_gate, ffn_w_up, ffn_w_down, w_in, conv_w, conv_b,
              dt_bias, a_log, d_skip, ssm_norm, q_norm, kv_norm, w_uq, w_ukv, w_out, pool_w,
              pool_scale, final_norm):
    rows = x.shape[1] // GRID_W
    cos, sin = axial_rope(rows)
    h, hc = x, ctx
    for l in range(DEPTH):
        j = l // 2
        even = l % 2 == 0
        ctx_out = any(k % 2 == 0 for k in range(l + 1, DEPTH))
        ctx_live = even or ctx_out
        m = [t[:, None, :] for t in adaln(c, mod_w[l], mod_b[l])]
        h = h + 0.5 * m[2] * swiglu(modulate(rmsnorm(h), m[0], m[1]),
                                    ffn_w_gate[l, 0], ffn_w_up[l, 0], ffn_w_down[l, 0])
        if ctx_live:
            mc = adaln(c_ctx, mod_w[l], mod_b[l])
            hc = hc + 0.5 * mc[2] * swiglu(modulate(rmsnorm(hc), mc[0], mc[1]),
                                           ffn_w_gate[l, 0], ffn_w_up[l, 0], ffn_w_down[l, 0])
            hcn = modulate(rmsnorm(hc), mc[3], mc[4])
        hn = modulate(rmsnorm(h), m[3], m[4])
        if even:
            mix, mix_c = ssd_mla_mixer(hn, hcn, ctx_out, cos, sin, w_in[j], conv_w[j], conv_b[j],
                                       dt_bias[j], a_log[j], d_skip[j], ssm_norm[j], q_norm[j],
                                       kv_norm[j], w_uq[j], w_ukv[j], w_out[j])
        else:
            mix = pool_mixer(hn, pool_w[j], pool_scale[j])
            mix_c = pool_mixer(hcn, pool_w[j], pool_scale[j]) if ctx_out else None
        h = h + m[5] * mix
        h = h + 0.5 * m[8] * swiglu(modulate(rmsnorm(h), m[6], m[7]),
                                    ffn_w_gate[l, 1], ffn_w_up[l, 1], ffn_w_down[l, 1])
        if ctx_out:
            hc = hc + mc[5] * mix_c
            hc = hc + 0.5 * mc[8] * swiglu(modulate(rmsnorm(hc), mc[6], mc[7]),
                                           ffn_w_gate[l, 1], ffn_w_up[l, 1], ffn_w_down[l, 1])
    return rmsnorm(h, final_norm)
```

```python
import numpy as np
from contextlib import ExitStack
import concourse.bass as bass
import concourse.mybir as mybir
from concourse.bass_utils import run_bass_kernel_spmd

F32 = mybir.dt.float32
BF16 = mybir.dt.bfloat16
AF = mybir.ActivationFunctionType
ALU = mybir.AluOpType
EPS = 1e-6


class Cfg:
    def __init__(self, **kw):
        self.D = 2048
        self.DFF = 5632
        self.NT = 2048
        self.NCX = 256
        self.TB = 512
        self.phases = "all"
        self.__dict__.update(kw)
        self.KC = self.D // 128
        self.NF = self.DFF // 128
        self.NTOK = self.NT + self.NCX
        self.NB = getattr(self, "NB", 1)
        self.NRP = 1
        while self.NRP < self.NB + 1:
            self.NRP *= 2


class Tracker:
    ENGS = ("pe", "act", "dve", "pool", "sp")

    def __init__(self, nc, es, n_dma_sems=12):
        self.nc = nc
        self.ops = {e: [] for e in self.ENGS}
        self.sem = {e: es.enter_context(nc.semaphore(f"sem_{e}")) for e in ("pe", "act", "dve", "pool")}
        self.cnt = {e: 0 for e in self.sem}
        self.dma = {q: [[es.enter_context(nc.semaphore(f"dsem_{q}{i}")), 0] for i in range(n_dma_sems)]
                    for q in ("sp", "pool")}
        self.dma_rr = {q: 0 for q in self.dma}
        self.cc_sem = es.enter_context(nc.semaphore("sem_cc"))
        self.cc_cnt = 0
        self.known = {e: {} for e in self.ENGS}
        self.state = {}
        self.sync_same = {"pe": False, "act": True, "dve": True, "pool": True, "sp": True}
        self.nops = 0

    def _need(self, eng, tok, waits):
        sem, val, teng, is_dma = tok
        if val <= 0:
            return
        if (not is_dma) and teng == eng and not self.sync_same[eng]:
            return
        k = id(sem)
        if self.known[eng].get(k, 0) >= val:
            return
        if k not in waits or waits[k][1] < val:
            waits[k] = (sem, val)

    def op(self, eng, meth, kw, reads=(), writes=(), signal=True, dma=False, cc=False):
        fn = (meth, kw)
        self.nops += 1
        waits = {}
        for key in reads:
            st = self.state.get(key)
            if st and st[0] is not None:
                self._need(eng, st[0], waits)
        for key in writes:
            st = self.state.get(key)
            if st:
                if st[0] is not None:
                    self._need(eng, st[0], waits)
                for t in st[1]:
                    self._need(eng, t, waits)
        if cc:
            self.cc_cnt += 1
            tok = (self.cc_sem, self.cc_cnt, eng, True)
            sig = (self.cc_sem, 1)
        elif dma:
            pool = self.dma[eng]
            ent = pool[self.dma_rr[eng] % len(pool)]
            self.dma_rr[eng] += 1
            if ent[1] > 0:
                self._need(eng, (ent[0], ent[1], eng, True), waits)
            ent[1] += 16
            tok = (ent[0], ent[1], eng, True)
            sig = (ent[0], 16)
        else:
            assert eng in self.sem
            if signal:
                self.cnt[eng] += 1
                tok = (self.sem[eng], self.cnt[eng], eng, False)
                sig = (self.sem[eng], 1)
            else:
                assert eng == "pe"
                tok = (self.sem[eng], self.cnt[eng] + 1, eng, False)
                sig = None
        wl = list(waits.values())
        for sem, val in wl:
            self.known[eng][id(sem)] = max(self.known[eng].get(id(sem), 0), val)
        self.ops[eng].append((wl, fn, sig))
        for key in writes:
            self.state[key] = [tok, []]
        for key in reads:
            if key in writes:
                continue
            st = self.state.setdefault(key, [None, []])
            st[1].append(tok)
        return tok

    def fence(self):
        toks = [(self.sem[e], self.cnt[e], e, False) for e in self.sem]
        for q in self.dma:
            for ent in self.dma[q]:
                toks.append((ent[0], ent[1], q, True))
        toks.append((self.cc_sem, self.cc_cnt, "cc", True))
        for eng in self.ENGS:
            waits = {}
            for t in toks:
                sem, val, teng, is_dma = t
                if val <= 0 or ((not is_dma) and teng == eng):
                    continue
                if self.known[eng].get(id(sem), 0) >= val:
                    continue
                waits[id(sem)] = (sem, val)
            wl = list(waits.values())
            for sem, val in wl:
                self.known[eng][id(sem)] = val
            if wl:
                self.ops[eng].append((wl, None, None))
        self.state = {}

    def flush(self):
        self.fence()
        self.emit()
        if not hasattr(self, "history"):
            self.history = []
        self.history.append(self.ops)
        self.ops = {e: [] for e in self.ENGS}

    def simulate(self):
        val = {}
        for pi, ops in enumerate(self.history + [self.ops]):
            ptr = {e: 0 for e in self.ENGS}
            while True:
                prog = False
                for e in self.ENGS:
                    while ptr[e] < len(ops[e]):
                        wl, fn, sig = ops[e][ptr[e]]
                        if all(val.get(id(sem), 0) >= v for sem, v in wl):
                            if sig is not None:
                                val[id(sig[0])] = val.get(id(sig[0]), 0) + sig[1]
                            ptr[e] += 1
                            prog = True
                        else:
                            break
                if all(ptr[e] == len(ops[e]) for e in self.ENGS):
                    break
                if not prog:
                    rep = [f"deadlock in phase {pi}"]
                    for e in self.ENGS:
                        if ptr[e] < len(ops[e]):
                            wl, fn, sig = ops[e][ptr[e]]
                            bad = [(getattr(sem, 'name', str(sem)), v, val.get(id(sem), 0)) for sem, v in wl if val.get(id(sem), 0) < v]
                            rep.append(f"  {e} stuck at op {ptr[e]}/{len(ops[e])}: {fn[0] if fn else None} waits {bad}")
                    return "\n".join(rep)
        return None

    def wait_tokens(self, eng, toks):
        waits = {}
        for t in toks:
            self._need(eng, t, waits)
        wl = list(waits.values())
        for sem, val in wl:
            self.known[eng][id(sem)] = max(self.known[eng].get(id(sem), 0), val)
        if wl:
            self.ops[eng].append((wl, None, None))

    def emit(self):
        nc = self.nc
        ops = self.ops

        def run(engine, lst):
            for wl, fn, sig in lst:
                for sem, val in wl:
                    engine.wait_ge(sem, val)
                if fn is not None:
                    ins = getattr(engine, fn[0])(*fn[1][0], **fn[1][1])
                    if sig is not None:
                        ins.then_inc(sig[0], sig[1])

        with nc.Block() as block:
            @block.tensor
            def _(e):
                run(e, ops["pe"])

            @block.scalar
            def _(e):
                run(e, ops["act"])

            @block.vector
            def _(e):
                run(e, ops["dve"])

            @block.gpsimd
            def _(e):
                run(e, ops["pool"])

            @block.sync
            def _(e):
                run(e, ops["sp"])


class Prog:
    def __init__(self, cfg):
        self.cfg = cfg
        self.nc = bass.Bass("TRN2", target_bir_lowering=False)
        self.es = ExitStack()
        self.T = Tracker(self.nc, self.es)
        self.uid = 0
        self.inputs = {}

    def din(self, name, shape, dtype=F32):
        t = self.nc.dram_tensor(name, list(shape), dtype, kind="ExternalInput").ap()
        self.inputs[name] = t
        return t

    def dout(self, name, shape, dtype=F32):
        return self.nc.dram_tensor(name, list(shape), dtype, kind="ExternalOutput").ap()

    def dscratch(self, name, shape, dtype=F32):
        return self.nc.dram_tensor(name, list(shape), dtype).ap()

    def sb(self, stack, name, shape, dtype):
        self.uid += 1
        return stack.enter_context(self.nc.sbuf_tensor(f"{name}_{self.uid}", list(shape), dtype))

    def ps(self, stack, name, shape, dtype=F32):
        self.uid += 1
        return stack.enter_context(self.nc.psum_tensor(f"{name}_{self.uid}", list(shape), dtype))


def C(*a, **k):
    return (a, k)


def fm(ap, p=128):
    return ap.rearrange("(c p) t -> p c t", p=p)


def phase_consts(P, G):
    cfg, T = P.cfg, P.T
    es = P.es
    G["ident"] = P.sb(es, "ident", [128, 128], F32)
    G["ones_bf"] = P.sb(es, "ones_bf", [128, 128], BF16)
    G["ones_f"] = P.sb(es, "ones_f", [128, 128], F32)
    G["eps"] = P.sb(es, "eps", [128, 1], F32)
    G["one"] = P.sb(es, "one", [128, 1], F32)
    ident_d = P.din("c_ident", [128, 128])
    T.op("sp", "dma_start", C(out=G["ident"][:], in_=ident_d[:, :]), writes=["ident"], dma=True)
    T.op("dve", "memset", C(G["ones_bf"][:], 1.0), writes=["ones_bf"])
    T.op("dve", "memset", C(G["ones_f"][:], 1.0), writes=["ones_f"])
    T.op("dve", "memset", C(G["eps"][:], EPS), writes=["eps"])
    T.op("dve", "memset", C(G["one"][:], 1.0), writes=["one"])
    G["mod"] = P.sb(es, "mod", [128, 2, 2, 9, cfg.KC], F32)


def phase_adaln(P, G):
    cfg, T, nc = P.cfg, P.T, P.nc
    D, KC, NB, NCR = cfg.D, cfg.KC, cfg.NB, cfg.ncores
    NR = 1
    while NR < NB + 1:
        NR *= 2
    NM = 9 * D
    NMS = NM // NCR
    CB = 256 if NMS % 256 == 0 else 128
    assert NMS % CB == 0
    nblk = NMS // CB
    ccT = P.din("ccT_all", [D, NR])
    mod_w = P.din("mod_w_s", [2, D, NMS])
    mod_b = P.din("mod_b_s", [2, NMS])
    sel_d = P.din("sel", [NR, 2])
    ag_in = P.dscratch("ag_in", [2 * NR, NMS])
    ag_out = P.dscratch("ag_out", [NCR * 2 * NR, NMS])
    nch = 9 * KC
    with ExitStack() as ph:
        cc = P.sb(ph, "cc", [128, KC, NR], F32)
        scc = P.sb(ph, "scc", [128, KC, NR], F32)
        wt = [P.sb(ph, f"wt{i}", [128, KC, CB], F32) for i in range(2)]
        mrow = [P.sb(ph, f"mrow{i}", [NR, CB], F32) for i in range(2)]
        mb = [P.sb(ph, f"mb{i}", [NR, CB], F32) for i in range(2)]
        ps = [P.ps(ph, f"aps{i}", [128, 512]) for i in range(2)]
        T.op("sp", "dma_start", C(out=cc[:], in_=ccT.rearrange("(c p) r -> p c r", p=128)), writes=["cc"], dma=True)
        T.op("act", "activation", C(out=scc[:], in_=cc[:], func=AF.Silu), reads=["cc"], writes=["scc"])
        n_ = 0
        for l in range(2):
            for j in range(nblk):
                jb = n_ % 2
                n_ += 1
                w = wt[jb]
                wk = ("wt", jb)
                q = "sp" if jb == 0 else "pool"
                T.op(q, "dma_start", C(out=w[:], in_=mod_w[l, :, j * CB:(j + 1) * CB].rearrange("(c p) n -> p c n", p=128)),
                     writes=[wk], dma=True)
                T.op("sp", "dma_start", C(out=mb[jb][:], in_=mod_b[l, j * CB:(j + 1) * CB].partition_broadcast(NR)),
                     writes=[("mb", jb)], dma=True)
                pk = ("aps", jb)
                p_ = ps[jb]
                for k in range(KC):
                    T.op("pe", "matmul", C(p_[0:NR, :CB], lhsT=scc[:, k, :], rhs=w[:, k, :], start=(k == 0), stop=(k == KC - 1)),
                         reads=[wk, "scc"], writes=[pk], signal=(k == KC - 1))
                T.op("dve", "tensor_tensor", C(out=mrow[jb][:], in0=p_[0:NR, :CB], in1=mb[jb][:], op=ALU.add),
                     reads=[pk, ("mb", jb)], writes=[("mrow", jb)])
                T.op("sp", "dma_start", C(out=ag_in[l * NR:(l + 1) * NR, j * CB:(j + 1) * CB], in_=mrow[jb][:]),
                     reads=[("mrow", jb)], writes=[("ag_in", l, j)], dma=True)
        T.flush()
    rg = [list(range(NCR))]
    T.op("pool", "collective_compute", C("AllGather", ALU.bypass, replica_groups=rg, ins=[ag_in], outs=[ag_out]), writes=["cc_ag"], cc=True)
    T.flush()
    with ExitStack() as ph:
        sel = P.sb(ph, "sel", [NR, 2], F32)
        mg = [P.sb(ph, f"mg{i}", [NR, NMS], F32) for i in range(2)]
        pst = [P.ps(ph, f"apst{i}", [128, 512]) for i in range(2)]
        T.op("sp", "dma_start", C(out=sel[:], in_=sel_d[:, :]), writes=["sel"], dma=True)
        n_ = 0
        for l in range(2):
            for q in range(NCR):
                gb = n_ % 2
                n_ += 1
                r0 = q * 2 * NR + l * NR
                T.op("sp", "dma_start", C(out=mg[gb][:], in_=ag_out[r0:r0 + NR, :]), writes=[("mg", gb)], dma=True)
                for i in range(NMS // 128):
                    jj = (q * NMS) // 128 + i
                    T.op("pe", "matmul", C(pst[l][:, 2 * jj:2 * jj + 2], lhsT=mg[gb][:, i * 128:(i + 1) * 128], rhs=sel[:, :],
                                           start=True, stop=True),
                         reads=[("mg", gb), "sel"], writes=[("apst", l)], signal=True)
            for r in range(2):
                T.op("dve", "tensor_copy", C(
                    out=G["mod"][:, l, r, :, :],
                    in_=pst[l][:, 0:2 * nch].rearrange("p (k c r) -> p r k c", k=9, c=KC, r=2)[:, r, :, :]),
                    reads=[("apst", l)], writes=[("mod", l, r)])
            for r in range(2):
                for k in (1, 4, 7):
                    T.op("dve", "tensor_scalar_add", C(out=G["mod"][:, l, r, k, :], in0=G["mod"][:, l, r, k, :], scalar1=1.0),
                         reads=[("mod", l, r)], writes=[("mod", l, r)])
                for k in (2, 8):
                    T.op("dve", "tensor_scalar_mul", C(out=G["mod"][:, l, r, k, :], in0=G["mod"][:, l, r, k, :], scalar1=0.5),
                         reads=[("mod", l, r)], writes=[("mod", l, r)])
        T.flush()


def phase_load_T(P, G, src, dst, ntok, t_off):
    cfg, T = P.cfg, P.T
    KC = cfg.KC
    with ExitStack() as ph:
        xin = [P.sb(ph, f"xin{i}", [128, cfg.D], F32) for i in range(2)]
        xo = [P.sb(ph, f"xo{i}", [128, KC, 128], F32) for i in range(2)]
        pt = [P.ps(ph, f"ltp{i}", [128, 512]) for i in range(4)]
        for ti in range(ntok // 128):
            b = ti % 2
            T.op("sp", "dma_start", C(out=xin[b][:], in_=src[ti * 128:(ti + 1) * 128, :]),
                 writes=[("xin", b)], dma=True)
            for g4 in range(KC // 4):
                pk = ("ltp", (ti * (KC // 4) + g4) % 4)
                p_ = pt[(ti * (KC // 4) + g4) % 4]
                for i in range(4):
                    c = g4 * 4 + i
                    T.op("pe", "transpose", C(
                        p_[:, i * 128:(i + 1) * 128], xin[b][:, c * 128:(c + 1) * 128], G["ident"][:]),
                        reads=[("xin", b), "ident"], writes=[pk], signal=(i == 3))
                eng = "dve" if g4 % 2 == 0 else "act"
                if eng == "dve":
                    T.op("dve", "tensor_copy", C(
                        out=xo[b][:, g4 * 4:(g4 + 1) * 4, :], in_=p_[:].rearrange("p (i t) -> p i t", i=4)),
                        reads=[pk], writes=[("xo", b, g4)])
                else:
                    T.op("act", "copy", C(
                        out=xo[b][:, g4 * 4:(g4 + 1) * 4, :], in_=p_[:].rearrange("p (i t) -> p i t", i=4)),
                        reads=[pk], writes=[("xo", b, g4)])
            T.op("sp", "dma_start", C(
                out=fm(dst)[:, :, t_off + ti * 128:t_off + (ti + 1) * 128], in_=xo[b][:]),
                reads=[("xo", b, g4) for g4 in range(KC // 4)], writes=[("hT_d", t_off + ti * 128)], dma=True)
        T.flush()


def token_blocks(cfg, with_ctx):
    blks = []
    t = 0
    while t < cfg.NT:
        n = min(cfg.TB, cfg.NT - t)
        blks.append((t, n, 0))
        t += n
    if with_ctx:
        t = cfg.NT
        while t < cfg.NTOK:
            n = min(cfg.TB, cfg.NTOK - t)
            blks.append((t, n, 1))
            t += n
    return blks


def phase_ffn(P, G, l, j, with_ctx):
    cfg, T = P.cfg, P.T
    D, KC, NF, TB = cfg.D, cfg.KC, cfg.NF, cfg.TB
    hT = G["hT_d"]
    wg_d, wu_d, wd_d = G["ffn_w_gate", l, j], G["ffn_w_up", l, j], G["ffn_w_down", l, j]
    k0 = 0 if j == 0 else 6
    GU = 2
    GD = 4
    HALF = D // 2
    NDH = HALF // 128
    assert NF % GU == 0 and NF % GD == 0
    with ExitStack() as ph:
        hb = P.sb(ph, "hb", [128, KC, TB], F32)
        ub = P.sb(ph, "ub", [128, KC, TB], BF16)
        a = P.sb(ph, "a", [128, NF, TB], BF16)
        bufs = {
            "sq": [P.sb(ph, f"sq{i}", [128, TB], BF16) for i in range(2)],
            "rstd": P.sb(ph, "rstd", [128, TB], F32),
            "tmp": [P.sb(ph, f"tmp{i}", [128, TB], F32) for i in range(2)],
        }
        sg = [P.sb(ph, f"sg{i}", [128, TB], F32) for i in range(2)]
        wg = [P.sb(ph, f"wg{i}", [128, KC, GU * 128], BF16) for i in range(2)]
        wu = [P.sb(ph, f"wu{i}", [128, KC, GU * 128], BF16) for i in range(2)]
        wd = [P.sb(ph, f"wd{i}", [128, GD, HALF], BF16) for i in range(2)]
        psb = [P.ps(ph, f"fps{i}", [128, 512]) for i in range(8)]
        bufs["ssq_ps"] = psb[0]
        PSK = [("fps", i) for i in range(8)]
        wcount = {"gu": 0, "d": 0}
        for (t0, tb, row) in token_blocks(cfg, with_ctx):
            mod = G["mod"]
            sh_ap, sc_ap, gt_ap = mod[:, l, row, k0, :], mod[:, l, row, k0 + 1, :], mod[:, l, row, k0 + 2, :]
            T.op("sp", "dma_start", C(out=hb[:, :, :tb], in_=fm(hT)[:, :, t0:t0 + tb]),
                 writes=[("fhb", c) for c in range(KC)], dma=True)
            bufs2 = dict(bufs)
            emit_norm_mod_ffn(P, G, bufs2, hb, ub, tb, sc_ap, sh_ap, PSK[0])
            for fg in range(NF // GU):
                wb = wcount["gu"] % 2
                wcount["gu"] += 1
                T.op("pool", "dma_start", C(
                    out=wg[wb][:], in_=wg_d[:, fg * GU * 128:(fg + 1) * GU * 128].rearrange(
                        "(c p) n -> p c n", p=128)), writes=[("wg", wb)], dma=True)
                T.op("pool", "dma_start", C(
                    out=wu[wb][:], in_=wu_d[:, fg * GU * 128:(fg + 1) * GU * 128].rearrange(
                        "(c p) n -> p c n", p=128)), writes=[("wu", wb)], dma=True)
                for fi in range(GU):
                    fc = fg * GU + fi
                    pg_i, pu_i = 1 + (fc % 2) * 2, 2 + (fc % 2) * 2
                    pg, pu = psb[pg_i], psb[pu_i]
                    for k in range(KC):
                        T.op("pe", "matmul", C(
                            pg[:, :tb], lhsT=wg[wb][:, k, fi * 128:(fi + 1) * 128], rhs=ub[:, k, :tb],
                            start=(k == 0), stop=(k == KC - 1)),
                            reads=[("wg", wb), ("fub", k)], writes=[PSK[pg_i]], signal=(k == KC - 1))
                    for k in range(KC):
                        T.op("pe", "matmul", C(
                            pu[:, :tb], lhsT=wu[wb][:, k, fi * 128:(fi + 1) * 128], rhs=ub[:, k, :tb],
                            start=(k == 0), stop=(k == KC - 1)),
                            reads=[("wu", wb), ("fub", k)], writes=[PSK[pu_i]], signal=(k == KC - 1))
                    s_ = sg[fc % 2]
                    T.op("act", "activation", C(out=s_[:, :tb], in_=pg[:, :tb], func=(AF.Identity if getattr(cfg, "nosilu", False) else AF.Silu)),
                         reads=[PSK[pg_i]], writes=[("sg", fc % 2)])
                    T.op("dve", "tensor_tensor", C(
                        out=a[:, fc, :tb], in0=s_[:, :tb], in1=pu[:, :tb], op=ALU.mult),
                        reads=[("sg", fc % 2), PSK[pu_i]], writes=[("a", fc)])
            for half in range(2):
                for fq in range(NF // GD):
                    wb = wcount["d"] % 2
                    wcount["d"] += 1
                    T.op("pool", "dma_start", C(
                        out=wd[wb][:], in_=wd_d[fq * GD * 128:(fq + 1) * GD * 128,
                                                half * HALF:(half + 1) * HALF].rearrange("(g p) n -> p g n", p=128)),
                        writes=[("wd", wb)], dma=True)
                    for fi in range(GD):
                        fc = fq * GD + fi
                        for dci in range(NDH):
                            T.op("pe", "matmul", C(
                                psb[dci][:, :tb], lhsT=wd[wb][:, fi, dci * 128:(dci + 1) * 128], rhs=a[:, fc, :tb],
                                start=(fc == 0), stop=(fc == NF - 1)),
                                reads=[("wd", wb), ("a", fc)], writes=[PSK[dci]],
                                signal=(fc == NF - 1 or (fi == GD - 1 and dci == NDH - 1)))
                for dci in range(NDH):
                    dc = half * NDH + dci
                    T.op("dve", "scalar_tensor_tensor", C(
                        out=hb[:, dc, :tb], in0=psb[dci][:, :tb], scalar=gt_ap[:, dc:dc + 1], in1=hb[:, dc, :tb],
                        op0=ALU.mult, op1=ALU.add),
                        reads=[PSK[dci], ("fhb", dc)], writes=[("fhb", dc)])
            T.op("sp", "dma_start", C(out=fm(hT)[:, :, t0:t0 + tb], in_=hb[:, :, :tb]),
                 reads=[("fhb", c) for c in range(KC)], writes=[("hT_d", t0)], dma=True)
            if getattr(cfg, "debug", False) and t0 == 0 and l == 0 and j == 0:
                dbg_u = P.dout("dbg_u", [128, KC * TB], BF16)
                dbg_a = P.dout("dbg_a", [128, NF * TB], BF16)
                T.op("sp", "dma_start", C(out=dbg_u[:, :], in_=ub[:].rearrange("p c t -> p (c t)")),
                     reads=[("fub", c) for c in range(KC)], writes=["dbgu"], dma=True)
                T.op("sp", "dma_start", C(out=dbg_a[:, :], in_=a[:].rearrange("p c t -> p (c t)")),
                     reads=[("a", c) for c in range(NF)], writes=["dbga"], dma=True)
        T.flush()


def emit_norm_mod_ffn(P, G, bufs, hb, ub, tb, sc_ap, sh_ap, ssq_key):
    cfg, T = P.cfg, P.T
    KC = cfg.KC
    sq, ssq_ps, rstd, tmp = bufs["sq"], bufs["ssq_ps"], bufs["rstd"], bufs["tmp"]
    for c in range(KC):
        s_ = sq[c % 2]
        T.op("act", "activation", C(out=s_[:, :tb], in_=hb[:, c, :tb], func=AF.Square),
             reads=[("fhb", c)], writes=[("sq", c % 2)])
        T.op("pe", "matmul", C(ssq_ps[:, :tb], lhsT=G["ones_bf"][:], rhs=s_[:, :tb],
                                                    start=(c == 0), stop=(c == KC - 1)),
             reads=[("sq", c % 2), "ones_bf"], writes=[ssq_key], signal=True)
    T.op("act", "activation", C(out=rstd[:, :tb], in_=ssq_ps[:, :tb], func=AF.Sqrt,
                                       scale=1.0 / cfg.D, bias=G["eps"][:, 0:1]),
         reads=[ssq_key, "eps"], writes=["rstd"])
    T.op("dve", "reciprocal", C(out=rstd[:, :tb], in_=rstd[:, :tb]), reads=["rstd"], writes=["rstd"])
    if ub is None:
        return
    for c in range(KC):
        t_ = tmp[c % 2]
        T.op("dve", "tensor_tensor", C(out=t_[:, :tb], in0=hb[:, c, :tb], in1=rstd[:, :tb],
                                                            op=ALU.mult),
             reads=[("fhb", c), "rstd"], writes=[("tmp", c % 2)])
        T.op("act", "activation", C(out=ub[:, c, :tb], in_=t_[:, :tb], func=AF.Identity,
                                                         scale=sc_ap[:, c:c + 1], bias=sh_ap[:, c:c + 1]),
             reads=[("tmp", c % 2)], writes=[("fub", c)])


def phase_final(P, G, out_d):
    cfg, T = P.cfg, P.T
    D, KC, TB = cfg.D, cfg.KC, cfg.TB
    hT = G["hT_d"]
    fn_d = P.din("final_norm", [D])
    with ExitStack() as ph:
        hb = P.sb(ph, "hb", [128, KC, TB], F32)
        gain = P.sb(ph, "gain", [128, KC], F32)
        bufs = {
            "sq": [P.sb(ph, f"sq{i}", [128, TB], BF16) for i in range(2)],
            "rstd": P.sb(ph, "rstd", [128, TB], F32),
            "tmp": [P.sb(ph, f"tmp{i}", [128, TB], F32) for i in range(2)],
            "ssq_ps": P.ps(ph, "ssq", [128, 512]),
        }
        yb = P.sb(ph, "yb", [128, KC, TB], F32)
        ot = [P.sb(ph, f"ot{i}", [128, D], F32) for i in range(2)]
        pt = [P.ps(ph, f"ftp{i}", [128, 512]) for i in range(4)]
        T.op("sp", "dma_start", C(out=gain[:], in_=fn_d.rearrange("(c p) -> p c", p=128),
                                         allow_slow_non_contiguous=True),
             writes=["gain"], dma=True)
        nt = 0
        for (t0, tb, row) in token_blocks(cfg, False):
            T.op("sp", "dma_start", C(out=hb[:, :, :tb], in_=fm(hT)[:, :, t0:t0 + tb]),
                 writes=[("fhb", c) for c in range(KC)], dma=True)
            emit_norm_mod_ffn(P, G, bufs, hb, None, tb, None, None, "ssq_ps")
            for c in range(KC):
                T.op("dve", "scalar_tensor_tensor", C(
                    out=yb[:, c, :tb], in0=hb[:, c, :tb], scalar=gain[:, c:c + 1], in1=bufs["rstd"][:, :tb],
                    op0=ALU.mult, op1=ALU.mult),
                    reads=[("fhb", c), "gain", "rstd"], writes=[("yb", c)])
            for ti in range(tb // 128):
                ob = nt % 2
                for g4 in range(KC // 4):
                    pi = (nt * (KC // 4) + g4) % 4
                    for i in range(4):
                        c = g4 * 4 + i
                        T.op("pe", "transpose", C(
                            pt[pi][:, i * 128:(i + 1) * 128], yb[:, c, ti * 128:(ti + 1) * 128], G["ident"][:]),
                            reads=[("yb", c), "ident"], writes=[("ftp", pi)], signal=(i == 3))
                    if g4 % 2 == 0:
                        T.op("dve", "tensor_copy", C(
                            out=ot[ob][:, g4 * 512:(g4 + 1) * 512], in_=pt[pi][:]),
                            reads=[("ftp", pi)], writes=[("ot", ob, g4)])
                    else:
                        T.op("act", "copy", C(
                            out=ot[ob][:, g4 * 512:(g4 + 1) * 512], in_=pt[pi][:]),
                            reads=[("ftp", pi)], writes=[("ot", ob, g4)])
                tok = T.op("sp", "dma_start", C(
                    out=out_d[t0 + ti * 128:t0 + (ti + 1) * 128, :], in_=ot[ob][:]),
                    reads=[("ot", ob, g4) for g4 in range(KC // 4)], writes=[("out", t0 + ti * 128)], dma=True)
                G["out_tokens"].append(tok)
                nt += 1
        T.flush()


QL, KVL, ROPE = 512, 512, 64
SSM_IN, SSM_H, SSM_P, SSM_G, SSM_N = 2048, 32, 64, 4, 128
MLA_H, NOPE, VH = 16, 128, 128
Q0, KV0, KR0, KRS0, Z0, X0, B0, C0 = 0, 512, 1024, 1088, 1152, 3200, 5248, 5760
NPROJ = 6272
MIXW = SSM_IN + MLA_H * VH
MLA_SCALE = (NOPE + ROPE) ** -0.5


def emit_rstd(P, G, src, R, tb, nfeat, bufs, src_keys, tag):
    T = P.T
    sq, ssq_ps, rstd = bufs["sq"], bufs["ssq_ps"], bufs["rstd"]
    for c in range(R):
        s_ = sq[c % 2]
        rk = src_keys[c] if isinstance(src_keys[c], list) else [src_keys[c]]
        T.op("act", "activation", C(out=s_[:, :tb], in_=src[:, c, :tb], func=AF.Square),
             reads=rk, writes=[(tag + "sq", c % 2)])
        T.op("pe", "matmul", C(ssq_ps[:, :tb], lhsT=G["ones_bf"][:], rhs=s_[:, :tb], start=(c == 0), stop=(c == R - 1)),
             reads=[(tag + "sq", c % 2), "ones_bf"], writes=[tag + "ssq"], signal=True)
    T.op("act", "activation", C(out=rstd[:, :tb], in_=ssq_ps[:, :tb], func=AF.Sqrt, scale=1.0 / nfeat,
                                bias=G["eps"][:, 0:1]), reads=[tag + "ssq", "eps"], writes=[tag + "rstd"])
    T.op("dve", "reciprocal", C(out=rstd[:, :tb], in_=rstd[:, :tb]), reads=[tag + "rstd"], writes=[tag + "rstd"])


def phase_inproj(P, G, l):
    cfg, T = P.cfg, P.T
    D, KC, TB = cfg.D, cfg.KC, cfg.TB
    hT, proj, dtd = G["hT_d"], G["proj_d"], G["dt_d"]
    w_in2, w_dt = G["w_in2"], G["w_dt"]
    groups = []
    for r0, n in ((Q0, 512), (KV0, 512)):
        groups += [(r0 + i, 256, [(0, 128), (128, 128)]) for i in range(0, n, 256)]
    groups.append((KR0, 128, [(0, 64), (64, 64)]))
    for r0, n in ((Z0, 2048), (X0, 2048), (B0, 512), (C0, 512)):
        groups += [(r0 + i, 256, [(0, 128), (128, 128)]) for i in range(0, n, 256)]
    with ExitStack() as ph:
        hb = P.sb(ph, "hb", [128, KC, TB], F32)
        ub = P.sb(ph, "ub", [128, KC, TB], BF16)
        bufs = {"sq": [P.sb(ph, f"sq{i}", [128, TB], BF16) for i in range(2)],
                "rstd": P.sb(ph, "rstd", [128, TB], F32),
                "tmp": [P.sb(ph, f"tmp{i}", [128, TB], F32) for i in range(2)],
                "ssq_ps": P.ps(ph, "ssq", [128, 512])}
        wt = [P.sb(ph, f"wt{i}", [128, KC, 256], BF16) for i in range(3)]
        wdt = P.sb(ph, "wdt", [128, KC, 64], BF16)
        st = [P.sb(ph, f"st{i}", [128, TB], F32) for i in range(3)]
        sdt = [P.sb(ph, f"sdt{i}", [128, 64], F32) for i in range(2)]
        pp = [P.ps(ph, f"ipp{i}", [128, 512]) for i in range(3)]
        pdt = P.ps(ph, "pdt", [128, 512])
        T.op("pool", "dma_start", C(out=wdt[:], in_=w_dt.rearrange("(c p) n -> p c n", p=128)), writes=["wdt"], dma=True)
        nw, ns, nd = 0, 0, 0
        mod = G["mod"]
        for (t0, tb, row) in token_blocks(cfg, True):
            sh_ap, sc_ap = mod[:, l, row, 3, :], mod[:, l, row, 4, :]
            T.op("sp", "dma_start", C(out=hb[:, :, :tb], in_=fm(hT)[:, :, t0:t0 + tb]),
                 writes=[("fhb", c) for c in range(KC)], dma=True)
            emit_norm_mod_ffn(P, G, bufs, hb, ub, tb, sc_ap, sh_ap, "ssq_ps")
            for (r0, wdth, subs) in groups:
                wb = nw % 3
                nw += 1
                T.op("pool", "dma_start", C(out=wt[wb][:, :, :wdth],
                                            in_=w_in2[:, r0:r0 + wdth].rearrange("(c p) n -> p c n", p=128)),
                     writes=[("ipw", wb)], dma=True)
                for (c0, m) in subs:
                    pi = ns % 3
                    ns += 1
                    for k in range(KC):
                        T.op("pe", "matmul", C(pp[pi][0:m, :tb], lhsT=wt[wb][:, k, c0:c0 + m], rhs=ub[:, k, :tb],
                                               start=(k == 0), stop=(k == KC - 1)),
                             reads=[("ipw", wb), ("fub", k)], writes=[("ipp", pi)], signal=(k == KC - 1))
                    if ns % 2 == 0:
                        T.op("dve", "tensor_copy", C(out=st[pi][0:m, :tb], in_=pp[pi][0:m, :tb]),
                             reads=[("ipp", pi)], writes=[("ipst", pi)])
                    else:
                        T.op("act", "copy", C(out=st[pi][0:m, :tb], in_=pp[pi][0:m, :tb]),
                             reads=[("ipp", pi)], writes=[("ipst", pi)])
                    T.op("sp", "dma_start", C(out=proj[r0 + c0:r0 + c0 + m, t0:t0 + tb], in_=st[pi][0:m, :tb]),
                         reads=[("ipst", pi)], writes=[("proj", r0 + c0, t0)], dma=True)
            for ti in range(tb // 128):
                di = nd % 2
                nd += 1
                for k in range(KC):
                    T.op("pe", "matmul", C(pdt[:, 0:64], lhsT=ub[:, k, ti * 128:(ti + 1) * 128], rhs=wdt[:, k, :],
                                           start=(k == 0), stop=(k == KC - 1)),
                         reads=["wdt", ("fub", k)], writes=["pdt"], signal=(k == KC - 1))
                T.op("dve", "tensor_copy", C(out=sdt[di][:], in_=pdt[:, 0:64]), reads=["pdt"], writes=[("sdt", di)])
                T.op("sp", "dma_start", C(out=dtd[t0 + ti * 128:t0 + (ti + 1) * 128, :], in_=sdt[di][:]),
                     reads=[("sdt", di)], writes=[("dtd", t0 + ti * 128)], dma=True)
        T.flush()


def phase_kvprep(P, G):
    cfg, T = P.cfg, P.T
    NT, NCX, TB = cfg.NT, cfg.NCX, cfg.TB
    proj = G["proj_d"]
    with ExitStack() as ph:
        src = [P.sb(ph, f"src{i}", [128, 4, TB], F32) for i in range(2)]
        dst = [P.sb(ph, f"dst{i}", [128, 4, TB], F32) for i in range(2)]
        bufs = {"sq": [P.sb(ph, f"sq{i}", [128, TB], BF16) for i in range(2)],
                "rstd": P.sb(ph, "rstd", [128, TB], F32),
                "ssq_ps": P.ps(ph, "ssq", [128, 512])}
        gq = P.sb(ph, "gq", [128, 4], F32)
        gkv = P.sb(ph, "gkv", [128, 4], F32)
        kr = P.sb(ph, "kr", [64, TB], F32)
        krs = P.sb(ph, "krs", [64, TB], F32)
        cc = P.sb(ph, "cc", [64, TB], F32)
        ss = P.sb(ph, "ss", [64, TB], F32)
        T.op("sp", "dma_start", C(out=gq[:], in_=G["q_norm"].rearrange("(c p) -> p c", p=128),
                                  allow_slow_non_contiguous=True), writes=["gq"], dma=True)
        T.op("sp", "dma_start", C(out=gkv[:], in_=G["kv_norm"].rearrange("(c p) -> p c", p=128),
                                  allow_slow_non_contiguous=True), writes=["gkv"], dma=True)
        n = 0
        for (t0, tb, row) in token_blocks(cfg, True):
            jobs = [("kv", KV0, gkv, "gkv")] + ([("q", Q0, gq, "gq")] if row == 0 else [])
            for (nm, r0, gain, gk) in jobs:
                b = n % 2
                n += 1
                T.op("sp", "dma_start", C(out=src[b][:, :, :tb], in_=fm(proj[r0:r0 + 512, :])[:, :, t0:t0 + tb]),
                     writes=[("src", b, c) for c in range(4)], dma=True)
                emit_rstd(P, G, src[b], 4, tb, 512, bufs, [("src", b, c) for c in range(4)], "kvp")
                for c in range(4):
                    T.op("dve", "scalar_tensor_tensor", C(out=dst[b][:, c, :tb], in0=src[b][:, c, :tb],
                                                          scalar=gain[:, c:c + 1], in1=bufs["rstd"][:, :tb],
                                                          op0=ALU.mult, op1=ALU.mult),
                         reads=[("src", b, c), gk, "kvprstd"], writes=[("dst", b, c)])
                if nm == "kv" and row == 0:
                    for c in range(4):
                        T.op("sp", "dma_start", C(out=G["ex1_in"][c][:, t0:t0 + tb], in_=dst[b][:, c, :tb]),
                             reads=[("dst", b, c)], writes=[("kvp_out", nm, t0, c)], dma=True)
                else:
                    if nm == "q":
                        o = fm(G["qcn_d"])[:, :, t0:t0 + tb]
                    else:
                        o = fm(G["kvn_ctx_d"])[:, :, t0 - NT:t0 - NT + tb]
                    T.op("sp", "dma_start", C(out=o, in_=dst[b][:, :, :tb]),
                         reads=[("dst", b, c) for c in range(4)], writes=[("kvp_out", nm, t0)], dma=True)
            if row == 0:
                T.op("sp", "dma_start", C(out=kr[:, :tb], in_=proj[KR0:KR0 + 64, t0:t0 + tb]), writes=["kr"], dma=True)
                T.op("sp", "dma_start", C(out=krs[:, :tb], in_=proj[KRS0:KRS0 + 64, t0:t0 + tb]), writes=["krs"], dma=True)
                T.op("sp", "dma_start", C(out=cc[:, :tb], in_=G["rope_cc"][:, t0:t0 + tb]), writes=["cc"], dma=True)
                T.op("sp", "dma_start", C(out=ss[:, :tb], in_=G["rope_ss"][:, t0:t0 + tb]), writes=["ss"], dma=True)
                T.op("dve", "tensor_tensor", C(out=kr[:, :tb], in0=kr[:, :tb], in1=cc[:, :tb], op=ALU.mult),
                     reads=["kr", "cc"], writes=["kr"])
                T.op("dve", "tensor_tensor", C(out=krs[:, :tb], in0=krs[:, :tb], in1=ss[:, :tb], op=ALU.mult),
                     reads=["krs", "ss"], writes=["krs"])
                T.op("dve", "tensor_tensor", C(out=kr[:, :tb], in0=kr[:, :tb], in1=krs[:, :tb], op=ALU.add),
                     reads=["kr", "krs"], writes=["kr"])
                T.op("sp", "dma_start", C(out=G["ex1_in"][4][:, t0:t0 + tb], in_=kr[:, :tb]),
                     reads=["kr"], writes=[("kvp_out", "kr", t0)], dma=True)
            else:
                T.op("sp", "dma_start", C(out=G["krc_d"][:, t0 - NT:t0 - NT + tb], in_=proj[KR0:KR0 + 64, t0:t0 + tb]),
                     writes=[("kvp_out", "krc", t0)], dma=True)
        T.op("sp", "dma_start", C(out=G["exchH_in"][:, 0:2], in_=proj[X0:X0 + 3072, 0:2]), writes=["exh0"], dma=True)
        T.op("sp", "dma_start", C(out=G["exchH_in"][:, 2:4], in_=proj[X0:X0 + 3072, NT - 2:NT]), writes=["exh1"], dma=True)
        T.flush()


def phase_exchange(P, G, pairs):
    cfg, T = P.cfg, P.T
    rg = [[2 * i, 2 * i + 1] for i in range(cfg.ncores // 2)]
    for (i_ap, o_ap) in pairs:
        T.op("pool", "collective_compute", C("AllGather", ALU.bypass, replica_groups=rg, ins=[i_ap], outs=[o_ap]),
             writes=["cc"], cc=True)
    T.flush()


def phase_attn(P, G):
    cfg, T = P.cfg, P.T
    NT, NCX = cfg.NT, cfg.NCX
    NK = NCX + 2 * NT
    NKT = NK // 128
    QB = min(512, NT)
    HG = 2
    kvsrc = [([G["kvn_ctx_d"][c * 128:(c + 1) * 128, :] for c in range(4)], G["krc_d"], NCX)]
    for r in range(2):
        kvsrc.append(([G["ex1_out"][c][r * 128:(r + 1) * 128, :] for c in range(4)], G["ex1_out"][4][r * 64:(r + 1) * 64, :], NT))
    with ExitStack() as ph:
        kvn = P.sb(ph, "kvn", [128, 4, NK], BF16)
        krT = P.sb(ph, "krT", [64, NK], BF16)
        qcn = P.sb(ph, "qcn", [128, 4, NT], BF16)
        cc = P.sb(ph, "cc", [64, NT], F32)
        ss = P.sb(ph, "ss", [64, NT], F32)
        KT = P.sb(ph, "KT", [128, HG, NK], BF16)
        V = P.sb(ph, "V", [128, NKT, HG * VH], BF16)
        qn = P.sb(ph, "qn", [128, HG, NT], BF16)
        qr = P.sb(ph, "qr", [64, HG, NT], BF16)
        wkv = P.sb(ph, "wkv", [128, 4, HG * 256], BF16)
        wqn = P.sb(ph, "wqn", [128, 4, HG * 128], BF16)
        wqr = P.sb(ph, "wqr", [128, 4, HG * 64], BF16)
        wqs = P.sb(ph, "wqs", [128, 4, HG * 64], BF16)
        t1 = [P.sb(ph, f"t1{i}", [64, QB], F32) for i in range(2)]
        t2 = [P.sb(ph, f"t2{i}", [64, QB], F32) for i in range(2)]
        pT = [P.sb(ph, f"pT{i}", [128, QB], BF16) for i in range(3)]
        rl = P.sb(ph, "rl", [128, QB], F32)
        ob = [P.sb(ph, f"ob{i}", [128, QB], BF16) for i in range(2)]
        psS = [P.ps(ph, f"psS{i}", [128, 512]) for i in range(3)]
        psO = P.ps(ph, "psO", [128, 512])
        psL = P.ps(ph, "psL", [128, 512])
        psX = [P.ps(ph, f"psX{i}", [128, 512]) for i in range(2)]
        off = 0
        for (kv_ap, kr_ap, n) in kvsrc:
            for c in range(4):
                T.op("pool", "dma_start", C(out=kvn[:, c, off:off + n], in_=kv_ap[c]), writes=[("kvn", off, c)], dma=True)
            T.op("pool", "dma_start", C(out=krT[:, off:off + n], in_=kr_ap), writes=[("krT", off)], dma=True)
            off += n
        kvn_keys = [("kvn", o_, c) for o_ in (0, NCX, NCX + NT) for c in range(4)]
        krT_keys = [("krT", 0), ("krT", NCX), ("krT", NCX + NT)]
        T.op("pool", "dma_start", C(out=qcn[:], in_=fm(G["qcn_d"])), writes=["qcn"], dma=True)
        T.op("sp", "dma_start", C(out=cc[:], in_=G["rope_cc"][:, :]), writes=["cc"], dma=True)
        T.op("sp", "dma_start", C(out=ss[:], in_=G["rope_ss"][:, :]), writes=["ss"], dma=True)
        nx = 0
        nS = 0
        for hg in range(MLA_H // HG):
            h0 = hg * HG
            T.op("pool", "dma_start", C(out=wkv[:], in_=fm(G["w_ukv"][:, h0 * 256:(h0 + HG) * 256])),
                 writes=["wkv"], dma=True)
            T.op("pool", "dma_start", C(out=wqn[:], in_=fm(G["w_uq_n"][:, h0 * 128:(h0 + HG) * 128])),
                 writes=["wqn"], dma=True)
            T.op("pool", "dma_start", C(out=wqr[:], in_=fm(G["w_uq_r"][:, h0 * 64:(h0 + HG) * 64])),
                 writes=["wqr"], dma=True)
            T.op("pool", "dma_start", C(out=wqs[:], in_=fm(G["w_uq_rs"][:, h0 * 64:(h0 + HG) * 64])),
                 writes=["wqs"], dma=True)
            for hi in range(HG):
                for k0 in range(0, NK, 512):
                    kn = min(512, NK - k0)
                    px = psX[nx % 2]; pk = ("psX", nx % 2); nx += 1
                    for c in range(4):
                        T.op("pe", "matmul", C(px[:, :kn], lhsT=wkv[:, c, hi * 256:hi * 256 + 128], rhs=kvn[:, c, k0:k0 + kn],
                                               start=(c == 0), stop=(c == 3)),
                             reads=["wkv"] + kvn_keys, writes=[pk], signal=(c == 3))
                    T.op("act", "copy", C(out=KT[:, hi, k0:k0 + kn], in_=px[:, :kn]), reads=[pk], writes=[("KT", hi)])
            for kt in range(NKT):
                px = psX[nx % 2]; pk = ("psX", nx % 2); nx += 1
                for c in range(4):
                    T.op("pe", "matmul", C(px[:, :HG * 128].rearrange("p (h x) -> p h x", x=128), lhsT=kvn[:, c, kt * 128:(kt + 1) * 128],
                                           rhs=wkv[:, c, :].rearrange("p (h x) -> p h x", x=256)[:, :, 128:256],
                                           start=(c == 0), stop=(c == 3)),
                         reads=["wkv"] + kvn_keys, writes=[pk], signal=(c == 3))
                T.op("dve", "tensor_copy", C(out=V[:, kt, :], in_=px[:, :HG * 128]), reads=[pk], writes=[("V", kt)])
            for hi in range(HG):
                for q0 in range(0, NT, QB):
                    px = psX[nx % 2]; pk = ("psX", nx % 2); nx += 1
                    for c in range(4):
                        T.op("pe", "matmul", C(px[:, :QB], lhsT=wqn[:, c, hi * 128:(hi + 1) * 128], rhs=qcn[:, c, q0:q0 + QB],
                                               start=(c == 0), stop=(c == 3)),
                             reads=["wqn", "qcn"], writes=[pk], signal=(c == 3))
                    T.op("act", "copy", C(out=qn[:, hi, q0:q0 + QB], in_=px[:, :QB]), reads=[pk], writes=[("qn", hi)])
                    px = psX[nx % 2]; pk = ("psX", nx % 2); nx += 1
                    tb_ = nx % 2
                    for c in range(4):
                        T.op("pe", "matmul", C(px[0:64, :QB], lhsT=wqr[:, c, hi * 64:(hi + 1) * 64], rhs=qcn[:, c, q0:q0 + QB],
                                               start=(c == 0), stop=(c == 3)),
                             reads=["wqr", "qcn"], writes=[pk], signal=(c == 3))
                    T.op("dve", "tensor_tensor", C(out=t1[tb_][:], in0=px[0:64, :QB], in1=cc[:, q0:q0 + QB], op=ALU.mult),
                         reads=[pk, "cc"], writes=[("t1", tb_)])
                    px = psX[nx % 2]; pk = ("psX", nx % 2); nx += 1
                    for c in range(4):
                        T.op("pe", "matmul", C(px[0:64, :QB], lhsT=wqs[:, c, hi * 64:(hi + 1) * 64], rhs=qcn[:, c, q0:q0 + QB],
                                               start=(c == 0), stop=(c == 3)),
                             reads=["wqs", "qcn"], writes=[pk], signal=(c == 3))
                    T.op("dve", "tensor_tensor", C(out=t2[tb_][:], in0=px[0:64, :QB], in1=ss[:, q0:q0 + QB], op=ALU.mult),
                         reads=[pk, "ss"], writes=[("t2", tb_)])
                    T.op("dve", "tensor_tensor", C(out=qr[:, hi, q0:q0 + QB], in0=t1[tb_][:], in1=t2[tb_][:], op=ALU.add),
                         reads=[("t1", tb_), ("t2", tb_)], writes=[("qr", hi)])
            for hi in range(HG):
                h = h0 + hi
                for q0 in range(0, NT, QB):
                    LA = 2
                    sis = {}

                    def emit_S(kt):
                        nonlocal nS
                        si = nS % 3
                        nS += 1
                        sis[kt] = si
                        T.op("pe", "matmul", C(psS[si][:, :QB], lhsT=KT[:, hi, kt * 128:(kt + 1) * 128], rhs=qn[:, hi, q0:q0 + QB],
                                               start=True, stop=False),
                             reads=[("KT", hi), ("qn", hi)], writes=[("psS", si)], signal=False)
                        T.op("pe", "matmul", C(psS[si][:, :QB], lhsT=krT[:, kt * 128:(kt + 1) * 128], rhs=qr[:, hi, q0:q0 + QB],
                                               start=False, stop=True),
                             reads=krT_keys + [("qr", hi)], writes=[("psS", si)], signal=True)
                        T.op("act", "activation", C(out=pT[si][:], in_=psS[si][:, :QB], func=AF.Exp, scale=MLA_SCALE),
                             reads=[("psS", si)], writes=[("pT", si)])

                    for kt in range(min(LA, NKT)):
                        emit_S(kt)
                    for kt in range(NKT):
                        si = sis[kt]
                        T.op("pe", "matmul", C(psO[:, :QB], lhsT=V[:, kt, hi * 128:(hi + 1) * 128], rhs=pT[si][:],
                                               start=(kt == 0), stop=(kt == NKT - 1)),
                             reads=[("V", kt), ("pT", si)], writes=["psO"], signal=False)
                        T.op("pe", "matmul", C(psL[:, :QB], lhsT=G["ones_bf"][:], rhs=pT[si][:],
                                               start=(kt == 0), stop=(kt == NKT - 1)),
                             reads=["ones_bf", ("pT", si)], writes=["psL"], signal=True)
                        if kt + LA < NKT:
                            emit_S(kt + LA)
                    T.op("dve", "reciprocal", C(out=rl[:], in_=psL[:, :QB]), reads=["psL"], writes=["rl"])
                    oi = (hi * (NT // QB) + q0 // QB) % 2
                    T.op("dve", "tensor_tensor", C(out=ob[oi][:], in0=psO[:, :QB], in1=rl[:], op=ALU.mult),
                         reads=["psO", "rl"], writes=[("ob", oi)])
                    T.op("sp", "dma_start", C(out=G["mixT_d"][SSM_IN + h * 128:SSM_IN + (h + 1) * 128, q0:q0 + QB], in_=ob[oi][:]),
                         reads=[("ob", oi)], writes=[("mix", h, q0)], dma=True)
        T.flush()


def phase_outproj(P, G, l):
    cfg, T = P.cfg, P.T
    D, KC, TB = cfg.D, cfg.KC, cfg.TB
    MC = MIXW // 128
    hT, mixT, w_out = G["hT_d"], G["mixT_d"], G["w_out"]
    with ExitStack() as ph:
        hb = P.sb(ph, "hb", [128, KC, TB], F32)
        mx = P.sb(ph, "mx", [128, MC, TB], BF16)
        wo = [P.sb(ph, f"wo{i}", [128, MC, 128], BF16) for i in range(2)]
        pp = [P.ps(ph, f"opp{i}", [128, 512]) for i in range(2)]
        gt = G["mod"][:, l, 0, 5, :]
        for (t0, tb, row) in token_blocks(cfg, False):
            T.op("sp", "dma_start", C(out=hb[:, :, :tb], in_=fm(hT)[:, :, t0:t0 + tb]),
                 writes=[("fhb", c) for c in range(KC)], dma=True)
            T.op("sp", "dma_start", C(out=mx[:, :, :tb], in_=fm(mixT)[:, :, t0:t0 + tb]), writes=["mx"], dma=True)
            for dc in range(KC):
                T.op("pool", "dma_start", C(out=wo[dc % 2][:], in_=fm(w_out[:, dc * 128:(dc + 1) * 128])),
                     writes=[("wo", dc % 2)], dma=True)
                for m in range(MC):
                    T.op("pe", "matmul", C(pp[dc % 2][:, :tb], lhsT=wo[dc % 2][:, m, :], rhs=mx[:, m, :tb],
                                           start=(m == 0), stop=(m == MC - 1)),
                         reads=[("wo", dc % 2), "mx"], writes=[("opp", dc % 2)], signal=(m == MC - 1))
                T.op("dve", "scalar_tensor_tensor", C(out=hb[:, dc, :tb], in0=pp[dc % 2][:, :tb], scalar=gt[:, dc:dc + 1],
                                                      in1=hb[:, dc, :tb], op0=ALU.mult, op1=ALU.add),
                     reads=[("opp", dc % 2), ("fhb", dc)], writes=[("fhb", dc)])
            T.op("sp", "dma_start", C(out=fm(hT)[:, :, t0:t0 + tb], in_=hb[:, :, :tb]),
                 reads=[("fhb", c) for c in range(KC)], writes=[("hT_d", t0)], dma=True)
        T.flush()


def phase_zero_rows(P, G, dst, nrows, ncols, dtype):
    T = P.T
    with ExitStack() as ph:
        z = P.sb(ph, "z", [128, ncols], dtype)
        T.op("dve", "memset", C(z[:], 0.0), writes=["z"])
        for r in range(0, nrows, 128):
            T.op("sp", "dma_start", C(out=dst[r:r + 128, 0:ncols], in_=z[:]), reads=["z"], writes=[("zr", r)], dma=True)
        T.flush()


def bc_last(ap2, n):
    shp = list(ap2.shape)
    return ap2.unsqueeze(len(shp)).to_broadcast(shp + [n])


def bc_mid(ap2, n):
    shp = list(ap2.shape)
    return ap2.unsqueeze(1).to_broadcast([shp[0], n, shp[1]])


def phase_conv(P, G):
    cfg, T = P.cfg, P.T
    NT, NCX, NTOK = cfg.NT, cfg.NCX, cfg.NTOK
    proj, xc, xtm, exH = G["proj_d"], G["xc_d"], G["xc_tm_d"], G["exchH_out"]
    NCC = 24
    with ExitStack() as ph:
        cw = P.sb(ph, "cw", [128, NCC, 5], F32)
        cb = P.sb(ph, "cb", [128, NCC], F32)
        fl = P.sb(ph, "fl", [128, 4], F32)
        xin = [P.sb(ph, f"xin{i}", [128, NT + 4], F32) for i in range(2)]
        xcn = [P.sb(ph, f"xcn{i}", [128, NCX + 4], F32) for i in range(2)]
        hl = [P.sb(ph, f"hl{i}", [128, 4], F32) for i in range(2)]
        hr = [P.sb(ph, f"hr{i}", [128, 4], F32) for i in range(2)]
        acc = [P.sb(ph, f"acc{i}", [128, NTOK], F32) for i in range(2)]
        yo = [P.sb(ph, f"yo{i}", [128, NTOK], F32) for i in range(2)]
        tt = [P.sb(ph, f"tt{i}", [128, 4, 128], F32) for i in range(2)]
        pt = [P.ps(ph, f"cvp{i}", [128, 512]) for i in range(2)]
        cwk = P.sb(ph, "cwk", [128, 5, NCC], F32)
        for k in range(5):
            T.op("sp", "dma_start", C(out=cwk[:, k, :], in_=G["conv_w"][k, :].rearrange("(c p) -> p c", p=128),
                                      allow_slow_non_contiguous=True), writes=[("cwk", k)], dma=True)
        T.op("dve", "tensor_copy", C(out=cw[:], in_=cwk[:].rearrange("p k c -> p c k")),
             reads=[("cwk", k) for k in range(5)], writes=["cw"])
        T.op("sp", "dma_start", C(out=cb[:], in_=G["conv_b"].rearrange("(c p) -> p c", p=128),
                                  allow_slow_non_contiguous=True), writes=["cb"], dma=True)
        T.op("sp", "dma_start", C(out=fl[:], in_=G["flags"][:, :]), writes=["fl"], dma=True)
        for i in range(2):
            T.op("dve", "memset", C(xcn[i][:], 0.0), writes=[("xcn", i)])
        ntp = 0
        for cc in range(NCC):
            b = cc % 2
            eng = "dve"
            r0 = X0 + cc * 128
            T.op("sp", "dma_start", C(out=xin[b][:, 2:NT + 2], in_=proj[r0:r0 + 128, 0:NT]), writes=[("xin", b)], dma=True)
            T.op("sp", "dma_start", C(out=xcn[b][:, 2:NCX + 2], in_=proj[r0:r0 + 128, NT:NTOK]), writes=[("xcn", b)], dma=True)
            T.op("sp", "dma_start", C(out=hl[b][:], in_=exH[cc * 128:(cc + 1) * 128, :]), writes=[("hl", b)], dma=True)
            T.op("sp", "dma_start", C(out=hr[b][:], in_=exH[3072 + cc * 128:3072 + (cc + 1) * 128, :]), writes=[("hr", b)], dma=True)
            T.op(eng, "tensor_scalar", C(out=xin[b][:, 0:2], in0=hl[b][:, 2:4], scalar1=fl[:, 0:1], scalar2=None, op0=ALU.mult),
                 reads=[("hl", b), "fl"], writes=[("xin", b)])
            T.op(eng, "tensor_scalar", C(out=xin[b][:, NT + 2:NT + 4], in0=hr[b][:, 0:2], scalar1=fl[:, 1:2], scalar2=None, op0=ALU.mult),
                 reads=[("hr", b), "fl"], writes=[("xin", b)])
            for (src, key, o0, n) in ((xin[b], ("xin", b), 0, NT), (xcn[b], ("xcn", b), NT, NCX)):
                T.op(eng, "tensor_scalar", C(out=acc[b][:, o0:o0 + n], in0=src[:, 0:n], scalar1=cw[:, cc, 0:1], scalar2=None, op0=ALU.mult),
                     reads=[key, "cw"], writes=[("acc", b)])
                for k in range(1, 5):
                    T.op(eng, "scalar_tensor_tensor", C(out=acc[b][:, o0:o0 + n], in0=src[:, k:k + n], scalar=cw[:, cc, k:k + 1],
                                                        in1=acc[b][:, o0:o0 + n], op0=ALU.mult, op1=ALU.add),
                         reads=[key, "cw", ("acc", b)], writes=[("acc", b)])
            T.op("act", "activation", C(out=yo[b][:], in_=acc[b][:], func=AF.Silu, bias=cb[:, cc:cc + 1]),
                 reads=[("acc", b), "cb"], writes=[("yo", b)])
            T.op("sp", "dma_start", C(out=xc[cc * 128:(cc + 1) * 128, :], in_=yo[b][:]), reads=[("yo", b)],
                 writes=[("xc", cc)], dma=True)
            if cc < 20:
                for t4 in range(0, NTOK // 128, 4):
                    nt4 = min(4, NTOK // 128 - t4)
                    pi = ntp % 2
                    ntp += 1
                    for i in range(nt4):
                        T.op("pe", "transpose", C(pt[pi][:, i * 128:(i + 1) * 128], yo[b][:, (t4 + i) * 128:(t4 + i + 1) * 128],
                                                  G["ident"][:]),
                             reads=[("yo", b), "ident"], writes=[("cvp", pi)], signal=(i == nt4 - 1))
                    T.op("act" if pi else "dve", "copy" if pi else "tensor_copy",
                         C(out=tt[pi][:, :nt4, :], in_=pt[pi][:, :nt4 * 128].rearrange("p (i c) -> p i c", c=128)),
                         reads=[("cvp", pi)], writes=[("tt", pi)])
                    T.op("sp", "dma_start", C(out=xtm[t4 * 128:(t4 + nt4) * 128, cc * 128:(cc + 1) * 128].rearrange(
                        "(i p) c -> p i c", p=128), in_=tt[pi][:, :nt4, :]),
                        reads=[("tt", pi)], writes=[("xtm", cc, t4)], dma=True)
        T.flush()


def phase_ssd_prep(P, G, stk):
    cfg, T = P.cfg, P.T
    NCH = cfg.NTOK // 128
    tabs = {}
    for nm in ("DT", "A", "WX", "ETOT"):
        tabs[nm] = P.sb(stk, nm, [128, NCH, 64], F32)
    for nm, src in (("tri_f", "c_tri_f"), ("tri_b", "c_tri_b"), ("negm_f", "c_negm_f"), ("negm_b", "c_negm_b")):
        shape = [128, 128] if nm.startswith("tri") else [128, 512]
        tabs[nm] = P.sb(stk, nm, shape, F32)
        T.op("sp", "dma_start", C(out=tabs[nm][:], in_=G[src][:, :]), writes=[nm], dma=True)
    tabs["negones"] = P.sb(stk, "negones", [128, 128], F32)
    T.op("dve", "memset", C(tabs["negones"][:], -1.0), writes=["negones"])
    tabs["fl"] = P.sb(stk, "flg", [128, 4], F32)
    T.op("sp", "dma_start", C(out=tabs["fl"][:], in_=G["flags"][:, :]), writes=["flg"], dma=True)
    tabs["dsk"] = P.sb(stk, "dsk", [128, 16], F32)
    T.op("sp", "dma_start", C(out=tabs["dsk"][:], in_=G["dsk"][:, :]), writes=["dsk"], dma=True)
    tabs["gssm"] = P.sb(stk, "gssm", [128, 16], F32)
    T.op("sp", "dma_start", C(out=tabs["gssm"][:], in_=G["ssm_norm"].rearrange("(c p) -> p c", p=128),
                              allow_slow_non_contiguous=True), writes=["gssm"], dma=True)
    G["tabs"] = tabs
    with ExitStack() as ph:
        dtb = P.sb(ph, "dtb", [128, 64], F32)
        aneg = P.sb(ph, "aneg", [128, 64], F32)
        raw = [P.sb(ph, f"raw{i}", [128, 64], F32) for i in range(2)]
        tots = [P.sb(ph, f"tots{i}", [128, 64], F32) for i in range(2)]
        dif = [P.sb(ph, f"dif{i}", [128, 64], F32) for i in range(2)]
        pcs = [P.ps(ph, f"pcs{i}", [128, 512]) for i in range(2)]
        ptt = [P.ps(ph, f"ptt{i}", [128, 512]) for i in range(2)]
        T.op("sp", "dma_start", C(out=dtb[:], in_=G["dt_bias"].partition_broadcast(128)), writes=["dtb"], dma=True)
        T.op("sp", "dma_start", C(out=aneg[:], in_=G["a_log"].partition_broadcast(128)), writes=["aneg"], dma=True)
        T.op("act", "activation", C(out=aneg[:], in_=aneg[:], func=AF.Exp), reads=["aneg"], writes=["aneg"])
        T.op("dve", "tensor_scalar_mul", C(out=aneg[:], in0=aneg[:], scalar1=-1.0), reads=["aneg"], writes=["aneg"])
        DT, A, WX, ETOT = tabs["DT"], tabs["A"], tabs["WX"], tabs["ETOT"]
        for c in range(NCH):
            b = c % 2
            T.op("sp", "dma_start", C(out=raw[b][:], in_=G["dt_d"][c * 128:(c + 1) * 128, :]), writes=[("raw", b)], dma=True)
            T.op("dve", "tensor_tensor", C(out=raw[b][:], in0=raw[b][:], in1=dtb[:], op=ALU.add),
                 reads=[("raw", b), "dtb"], writes=[("raw", b)])
            T.op("act", "activation", C(out=raw[b][:], in_=raw[b][:], func=AF.Exp), reads=[("raw", b)], writes=[("raw", b)])
            T.op("act", "activation", C(out=DT[:, c, :], in_=raw[b][:], func=AF.Ln, bias=G["one"][:, 0:1]),
                 reads=[("raw", b), "one"], writes=[("DT", c)])
            T.op("dve", "tensor_tensor", C(out=A[:, c, :], in0=DT[:, c, :], in1=aneg[:], op=ALU.mult),
                 reads=[("DT", c), "aneg"], writes=[("A", c)])
            for d, tri in ((0, "tri_f"), (1, "tri_b")):
                T.op("pe", "matmul", C(pcs[b][:, d * 32:(d + 1) * 32], lhsT=tabs[tri][:], rhs=A[:, c, d * 32:(d + 1) * 32],
                                       start=True, stop=True),
                     reads=[tri, ("A", c)], writes=[("pcs", b)])
            T.op("pe", "matmul", C(ptt[b][:, 0:64], lhsT=G["ones_f"][:], rhs=A[:, c, :], start=True, stop=True),
                 reads=["ones_f", ("A", c)], writes=[("ptt", b)])
            T.op("act", "activation", C(out=ETOT[:, c, :], in_=ptt[b][:, 0:64], func=AF.Exp), reads=[("ptt", b)], writes=[("ETOT", c)])
            T.op("act", "copy", C(out=tots[b][:], in_=ptt[b][:, 0:64]), reads=[("ptt", b)], writes=[("tots", b)])
            T.op("dve", "tensor_tensor", C(out=dif[b][:], in0=tots[b][:], in1=pcs[b][:, 0:64], op=ALU.subtract),
                 reads=[("tots", b), ("pcs", b)], writes=[("dif", b)])
            T.op("act", "activation", C(out=dif[b][:], in_=dif[b][:], func=AF.Exp), reads=[("dif", b)], writes=[("dif", b)])
            T.op("dve", "tensor_tensor", C(out=WX[:, c, :], in0=DT[:, c, :], in1=dif[b][:], op=ALU.mult),
                 reads=[("DT", c), ("dif", b)], writes=[("WX", c)])
        T.flush()


def ssd_state_step(P, G, bufs, g, d, c, S, skey, n_):
    T = P.T
    tabs = G["tabs"]
    xtm = G["xc_tm_d"]
    b = n_ % 2
    h0 = d * 32 + g * 8
    xt, bt, xdd, pst = bufs["xt"][b], bufs["bt"][b], bufs["xdd"][b], bufs["pst"]
    T.op("sp", "dma_start", C(out=xt[:], in_=xtm[c * 128:(c + 1) * 128, g * 512:(g + 1) * 512]), writes=[("xt", b)], dma=True)
    T.op("pool", "dma_start", C(out=bt[:], in_=xtm[c * 128:(c + 1) * 128, 2048 + g * 128:2048 + (g + 1) * 128]),
         writes=[("bt", b)], dma=True)
    T.op("dve", "tensor_tensor", C(out=xdd[:].rearrange("p (h x) -> p h x", x=64), in0=xt[:].rearrange("p (h x) -> p h x", x=64),
                                   in1=bc_last(tabs["WX"][:, c, h0:h0 + 8], 64), op=ALU.mult),
         reads=[("xt", b), ("WX", c)], writes=[("xdd", b)])
    T.op("pe", "matmul", C(pst[:, :], lhsT=bt[:], rhs=xdd[:], start=True, stop=True),
         reads=[("bt", b), ("xdd", b)], writes=["pst"])
    T.op("dve", "tensor_tensor", C(out=S[:].rearrange("p (h x) -> p h x", x=64), in0=S[:].rearrange("p (h x) -> p h x", x=64),
                                   in1=bc_last(tabs["ETOT"][:, c, h0:h0 + 8], 64), op=ALU.mult),
         reads=[skey, ("ETOT", c)], writes=[skey])
    T.op("dve", "tensor_tensor", C(out=S[:], in0=S[:], in1=pst[:, :], op=ALU.add), reads=[skey, "pst"], writes=[skey])


def phase_ssd0(P, G):
    cfg, T = P.cfg, P.T
    NL, NCH = cfg.NT // 128, cfg.NTOK // 128
    with ExitStack() as ph:
        bufs = {"xt": [P.sb(ph, f"xt{i}", [128, 512], F32) for i in range(2)],
                "bt": [P.sb(ph, f"bt{i}", [128, 128], BF16) for i in range(2)],
                "xdd": [P.sb(ph, f"xdd{i}", [128, 512], BF16) for i in range(2)],
                "pst": P.ps(ph, "pst", [128, 512])}
        Ss = [P.sb(ph, f"S{i}", [128, 512], F32) for i in range(2)]
        n_ = 0
        for g in range(SSM_G):
            for d in range(2):
                S = Ss[(g * 2 + d) % 2]
                skey = ("S", (g * 2 + d) % 2)
                T.op("dve", "memset", C(S[:], 0.0), writes=[skey])
                cx = list(range(NL, NCH))
                lat = list(range(NL))
                if d == 1:
                    cx, lat = cx[::-1], lat[::-1]
                for c in cx:
                    ssd_state_step(P, G, bufs, g, d, c, S, skey, n_)
                    n_ += 1
                T.op("sp", "dma_start", C(out=G["sctx_d"][(d * 4 + g) * 128:(d * 4 + g + 1) * 128, :], in_=S[:]),
                     reads=[skey], writes=[("sctx", d, g)], dma=True)
                for c in lat:
                    ssd_state_step(P, G, bufs, g, d, c, S, skey, n_)
                    n_ += 1
                T.op("sp", "dma_start", C(out=G["ex2_in"][d][g * 128:(g + 1) * 128, :], in_=S[:]),
                     reads=[skey], writes=[("ex2", d, g)], dma=True)
        T.flush()


def phase_ssd1(P, G):
    cfg, T = P.cfg, P.T
    NT, TB = cfg.NT, cfg.TB
    NL = NT // 128
    tabs = G["tabs"]
    xc, xtm, proj = G["xc_d"], G["xc_tm_d"], G["proj_d"]
    fl = tabs["fl"]
    with ExitStack() as ph:
        bufs = {"xt": [P.sb(ph, f"xt{i}", [128, 512], F32) for i in range(2)],
                "bt": [P.sb(ph, f"bt{i}", [128, 128], BF16) for i in range(2)],
                "xdd": [P.sb(ph, f"xdd{i}", [128, 512], BF16) for i in range(2)],
                "pst": P.ps(ph, "pst", [128, 512])}
        gated = P.sb(ph, "gated", [128, 4, NT], F32)
        Sb = P.sb(ph, "Sb", [128, NL, 512], BF16)
        CT = P.sb(ph, "CT", [128, NT], BF16)
        BT = P.sb(ph, "BT", [128, NT], BF16)
        S = P.sb(ph, "S", [128, 512], F32)
        S2 = P.sb(ph, "S2", [128, 512], F32)
        Sf = [P.sb(ph, f"Sf{i}", [128, 512], BF16) for i in range(2)]
        CBs = [P.sb(ph, f"CBs{i}", [128, 128], BF16) for i in range(2)]
        aT = [P.sb(ph, f"aT{i}", [128, 8, 128], F32) for i in range(2)]
        ex = [P.sb(ph, f"ex{i}", [128, 512], F32) for i in range(4)]
        M = [P.sb(ph, f"M{i}", [128, 8, 128], BF16) for i in range(2)]
        Cs = [P.sb(ph, f"Cs{i}", [128, 8, 128], BF16) for i in range(2)]
        xd = [P.sb(ph, f"xd{i}", [128, 512], BF16) for i in range(2)]
        xT = [P.sb(ph, f"xT{i}", [128, 4, 128], F32) for i in range(2)]
        zT = [P.sb(ph, f"zT{i}", [128, 4, 128], F32) for i in range(2)]
        yv = [P.sb(ph, f"yv{i}", [128, 4, 128], F32) for i in range(2)]
        ob = [P.sb(ph, f"ob{i}", [128, 4, TB], BF16) for i in range(2)]
        nbufs = {"sq": [P.sb(ph, f"sq{i}", [128, TB], BF16) for i in range(2)],
                 "rstd": P.sb(ph, "rstd", [128, TB], F32),
                 "ssq_ps": P.ps(ph, "ssq", [128, 512])}
        pD = [P.ps(ph, f"pD{i}", [128, 512]) for i in range(2)]
        pE = [P.ps(ph, f"pE{i}", [128, 512]) for i in range(2)]
        pCB = P.ps(ph, "pCB", [128, 512])
        pY = P.ps(ph, "pY", [128, 512])
        n_ = 0
        nD = 0
        for g in range(SSM_G):
            T.op("pool", "dma_start", C(out=BT[:], in_=xc[2048 + g * 128:2048 + (g + 1) * 128, 0:NT]), writes=["BT"], dma=True)
            T.op("pool", "dma_start", C(out=CT[:], in_=xc[2560 + g * 128:2560 + (g + 1) * 128, 0:NT]), writes=["CT"], dma=True)
            def load_init(dst, dkey, d, own_flag_col, other_flag_col, rank):
                T.op("sp", "dma_start", C(out=dst[:], in_=G["sctx_d"][(d * 4 + g) * 128:(d * 4 + g + 1) * 128, :]),
                     writes=[dkey], dma=True)
                r0 = rank * 512 + g * 128
                T.op("sp", "dma_start", C(out=S2[:], in_=G["ex2_out"][d][r0:r0 + 128, :]), writes=["S2"], dma=True)
                T.op("dve", "tensor_scalar", C(out=dst[:], in0=dst[:], scalar1=fl[:, own_flag_col:own_flag_col + 1], scalar2=None,
                                               op0=ALU.mult), reads=[dkey, "flg"], writes=[dkey])
                T.op("dve", "scalar_tensor_tensor", C(out=dst[:], in0=S2[:], scalar=fl[:, other_flag_col:other_flag_col + 1],
                                                      in1=dst[:], op0=ALU.mult, op1=ALU.add),
                     reads=["S2", dkey, "flg"], writes=[dkey])
            load_init(S, "S", 1, 0, 1, 1)
            for c in range(NL - 1, -1, -1):
                T.op("act", "copy", C(out=Sb[:, c, :], in_=S[:]), reads=["S"], writes=[("Sb", c)])
                if c > 0:
                    ssd_state_step(P, G, bufs, g, 1, c, S, "S", n_)
                    n_ += 1
            load_init(S, "S", 0, 1, 0, 0)
            for c in range(NL):
                cb_ = c % 2
                cs_ = slice(c * 128, (c + 1) * 128)
                T.op("act", "copy", C(out=Sf[cb_][:], in_=S[:]), reads=["S"], writes=[("Sf", cb_)])
                T.op("sp", "dma_start", C(out=xT[cb_][:], in_=fm(xc[g * 512:(g + 1) * 512, :])[:, :, cs_]), writes=[("xT", cb_)], dma=True)
                T.op("sp", "dma_start", C(out=zT[cb_][:], in_=fm(proj[Z0 + g * 512:Z0 + (g + 1) * 512, :])[:, :, cs_]),
                     writes=[("zT", cb_)], dma=True)
                T.op("pe", "matmul", C(pCB[:, 0:128], lhsT=BT[:, cs_], rhs=CT[:, cs_], start=True, stop=True),
                     reads=["BT", "CT"], writes=["pCB"])
                T.op("act", "copy", C(out=CBs[cb_][:], in_=pCB[:, 0:128]), reads=["pCB"], writes=[("CBs", cb_)])
                b = n_ % 2
                for d in range(2):
                    h0 = d * 32 + g * 8
                    tri = tabs["tri_f" if d == 0 else "tri_b"]
                    trik = "tri_f" if d == 0 else "tri_b"
                    negm = tabs["negm_f" if d == 0 else "negm_b"]
                    negk = "negm_f" if d == 0 else "negm_b"
                    T.op("pool", "tensor_tensor", C(out=aT[d][:], in0=bc_mid(tri[:], 8), in1=bc_last(tabs["A"][:, c, h0:h0 + 8], 128),
                                                    op=ALU.mult), reads=[trik, ("A", c)], writes=[("aT", d)])
                    for half in range(2):
                        di = nD % 2
                        nD += 1
                        hs = slice(half * 4, half * 4 + 4)
                        aTh = aT[d][:, hs, :].rearrange("p h l -> p (h l)")
                        T.op("pe", "matmul", C(pD[di][:, :], lhsT=G["ones_f"][:], rhs=aTh, start=True, stop=False),
                             reads=["ones_f", ("aT", d)], writes=[("pD", di)], signal=False)
                        for hh in range(4):
                            T.op("pe", "matmul", C(pD[di][:, hh * 128:(hh + 1) * 128], lhsT=aT[d][:, half * 4 + hh, :],
                                                   rhs=tabs["negones"][:], start=False, stop=False),
                                 reads=["negones", ("aT", d)], writes=[("pD", di)], signal=False)
                        T.op("pe", "matmul", C(pD[di][:, :], lhsT=G["ident"][:], rhs=negm[:], start=False, stop=True),
                             reads=["ident", negk], writes=[("pD", di)], signal=True)
                        T.op("pe", "matmul", C(pE[di][:, :], lhsT=G["ones_f"][:], rhs=aTh, start=True, stop=True),
                             reads=["ones_f", ("aT", d)], writes=[("pE", di)], signal=True)
                        T.op("act", "activation", C(out=ex[di][:], in_=pD[di][:, :], func=AF.Exp), reads=[("pD", di)], writes=[("ex", di)])
                        T.op("dve", "tensor_tensor", C(out=M[d][:, hs, :], in0=ex[di][:].rearrange("p (h l) -> p h l", l=128),
                                                       in1=bc_mid(CBs[cb_][:], 4), op=ALU.mult),
                             reads=[("ex", di), ("CBs", cb_)], writes=[("M", d)])
                        T.op("act", "activation", C(out=ex[2 + di][:], in_=pE[di][:, :], func=AF.Exp), reads=[("pE", di)], writes=[("ex", 2 + di)])
                        T.op("dve", "tensor_tensor", C(out=Cs[d][:, hs, :], in0=ex[2 + di][:].rearrange("p (h l) -> p h l", l=128),
                                                       in1=bc_mid(CT[:, cs_], 4), op=ALU.mult),
                             reads=[("ex", 2 + di), "CT"], writes=[("Cs", d)])
                xt = bufs["xt"][b]
                T.op("sp", "dma_start", C(out=xt[:], in_=xtm[c * 128:(c + 1) * 128, g * 512:(g + 1) * 512]), writes=[("xt", b)], dma=True)
                for d in range(2):
                    h0 = d * 32 + g * 8
                    T.op("dve", "tensor_tensor", C(out=xd[d][:].rearrange("p (h x) -> p h x", x=64),
                                                   in0=xt[:].rearrange("p (h x) -> p h x", x=64),
                                                   in1=bc_last(tabs["DT"][:, c, h0:h0 + 8], 64), op=ALU.mult),
                         reads=[("xt", b), ("DT", c)], writes=[("xd", d)])
                for h in range(8):
                    po = (h % 2) * 64
                    cs2 = slice((h // 2) * 128, (h // 2 + 1) * 128)
                    hx = slice(h * 64, (h + 1) * 64)
                    oy = pY[po:po + 64, cs2]
                    T.op("pe", "matmul", C(oy, lhsT=xd[0][:, hx], rhs=M[0][:, h, :], start=True, stop=False),
                         reads=[("xd", 0), ("M", 0)], writes=["pY"], signal=False)
                    T.op("pe", "matmul", C(oy, lhsT=Sf[cb_][:, hx], rhs=Cs[0][:, h, :], start=False, stop=False),
                         reads=[("Sf", cb_), ("Cs", 0)], writes=["pY"], signal=False)
                    T.op("pe", "matmul", C(oy, lhsT=xd[1][:, hx], rhs=M[1][:, h, :], start=False, stop=False),
                         reads=[("xd", 1), ("M", 1)], writes=["pY"], signal=False)
                    T.op("pe", "matmul", C(oy, lhsT=Sb[:, c, hx], rhs=Cs[1][:, h, :], start=False, stop=True),
                         reads=[("Sb", c), ("Cs", 1)], writes=["pY"], signal=(h == 7))
                for j in range(4):
                    T.op("dve", "scalar_tensor_tensor", C(out=yv[cb_][:, j, :], in0=xT[cb_][:, j, :],
                                                          scalar=tabs["dsk"][:, g * 4 + j:g * 4 + j + 1],
                                                          in1=pY[:, j * 128:(j + 1) * 128], op0=ALU.mult, op1=ALU.add),
                         reads=[("xT", cb_), "dsk", "pY"], writes=[("yv", cb_, j)])
                T.op("act", "activation", C(out=zT[cb_][:], in_=zT[cb_][:], func=AF.Silu), reads=[("zT", cb_)], writes=[("zT", cb_)])
                T.op("dve", "tensor_tensor", C(out=gated[:, :, cs_], in0=yv[cb_][:], in1=zT[cb_][:], op=ALU.mult),
                     reads=[("yv", cb_, j) for j in range(4)] + [("zT", cb_)], writes=[("gated", c)])
                if c < NL - 1:
                    h0 = g * 8
                    bt, xdd, pst = bufs["bt"][b], bufs["xdd"][b], bufs["pst"]
                    T.op("pool", "dma_start", C(out=bt[:], in_=xtm[c * 128:(c + 1) * 128, 2048 + g * 128:2048 + (g + 1) * 128]),
                         writes=[("bt", b)], dma=True)
                    T.op("dve", "tensor_tensor", C(out=xdd[:].rearrange("p (h x) -> p h x", x=64),
                                                   in0=xt[:].rearrange("p (h x) -> p h x", x=64),
                                                   in1=bc_last(tabs["WX"][:, c, h0:h0 + 8], 64), op=ALU.mult),
                         reads=[("xt", b), ("WX", c)], writes=[("xdd", b)])
                    T.op("pe", "matmul", C(pst[:, :], lhsT=bt[:], rhs=xdd[:], start=True, stop=True),
                         reads=[("bt", b), ("xdd", b)], writes=["pst"])
                    T.op("dve", "tensor_tensor", C(out=S[:].rearrange("p (h x) -> p h x", x=64), in0=S[:].rearrange("p (h x) -> p h x", x=64),
                                                   in1=bc_last(tabs["ETOT"][:, c, h0:h0 + 8], 64), op=ALU.mult),
                         reads=["S", ("ETOT", c)], writes=["S"])
                    T.op("dve", "tensor_tensor", C(out=S[:], in0=S[:], in1=pst[:, :], op=ALU.add), reads=["S", "pst"], writes=["S"])
                n_ += 1
            for (t0, tb, row) in token_blocks(cfg, False):
                oi = (t0 // TB) % 2
                gsl = gated[:, :, t0:t0 + tb]
                gk = [("gated", cc_) for cc_ in range(t0 // 128, (t0 + tb) // 128)]
                emit_rstd(P, G, gsl, 4, tb, 512, nbufs, [gk] * 4, "gn")
                for j in range(4):
                    T.op("dve", "scalar_tensor_tensor", C(out=ob[oi][:, j, :tb], in0=gated[:, j, t0:t0 + tb],
                                                          scalar=tabs["gssm"][:, g * 4 + j:g * 4 + j + 1],
                                                          in1=nbufs["rstd"][:, :tb], op0=ALU.mult, op1=ALU.mult),
                         reads=[("gated", cc_) for cc_ in range(t0 // 128, (t0 + tb) // 128)] + ["gssm", "gnrstd"],
                         writes=[("gob", oi)])
                T.op("sp", "dma_start", C(out=fm(G["mixT_d"][g * 512:(g + 1) * 512, :])[:, :, t0:t0 + tb], in_=ob[oi][:, :, :tb]),
                     reads=[("gob", oi)], writes=[("mixs", g, t0)], dma=True)
        T.flush()


POOL_W = (2, 4, 8, 16)


def phase_pool_a(P, G, l):
    cfg, T = P.cfg, P.T
    D, KC, TB, NT = cfg.D, cfg.KC, cfg.TB, cfg.NT
    hT, hn1 = G["hT_d"], G["hn1_d"]
    with ExitStack() as ph:
        hb = P.sb(ph, "hb", [128, KC, TB], F32)
        ub = P.sb(ph, "ubf", [128, KC, TB], F32)
        bufs = {"sq": [P.sb(ph, f"sq{i}", [128, TB], BF16) for i in range(2)],
                "rstd": P.sb(ph, "rstd", [128, TB], F32),
                "tmp": [P.sb(ph, f"tmp{i}", [128, TB], F32) for i in range(2)],
                "ssq_ps": P.ps(ph, "ssq", [128, 512])}
        mod = G["mod"]
        toks = []
        for (t0, tb, row) in token_blocks(cfg, False):
            T.op("sp", "dma_start", C(out=hb[:, :, :tb], in_=fm(hT)[:, :, t0:t0 + tb]),
                 writes=[("fhb", c) for c in range(KC)], dma=True)
            emit_norm_mod_ffn(P, G, bufs, hb, ub, tb, mod[:, l, 0, 4, :], mod[:, l, 0, 3, :], "ssq_ps")
            T.op("sp", "dma_start", C(out=fm(hn1)[:, :, t0:t0 + tb], in_=ub[:, :, :tb]),
                 reads=[("fub", c) for c in range(KC)], writes=[("hn1", t0)], dma=True)
        T.flush()
        T.op("sp", "dma_start", C(out=G["exch3_in"][:, 0:8], in_=hn1[:, 0:8]), writes=["e3a"], dma=True)
        T.op("sp", "dma_start", C(out=G["exch3_in"][:, 8:16], in_=hn1[:, NT - 8:NT]), writes=["e3b"], dma=True)
        T.flush()


def phase_pool_b(P, G, l):
    cfg, T = P.cfg, P.T
    D, KC, NT = cfg.D, cfg.KC, cfg.NT
    GC = KC // 4
    L = NT + 16
    QB = min(512, NT)
    hT, hn1, ex3 = G["hT_d"], G["hn1_d"], G["exch3_out"]
    with ExitStack() as ph:
        fl = P.sb(ph, "fl", [128, 4], F32)
        rc = P.sb(ph, "rc", [128, 4, NT], F32)
        gs = P.sb(ph, "gs", [128, KC], F32)
        ext = [P.sb(ph, f"ext{i}", [128, L], F32) for i in range(2)]
        hl = [P.sb(ph, f"hl{i}", [128, 16], F32) for i in range(2)]
        hr = [P.sb(ph, f"hr{i}", [128, 16], F32) for i in range(2)]
        sa = P.sb(ph, "sa", [128, L], F32)
        sb_ = P.sb(ph, "sbb", [128, L], F32)
        pooled = P.sb(ph, "pooled", [128, GC, NT], BF16)
        wp = P.sb(ph, "wp", [128, GC, GC * 128], BF16)
        hc = [P.sb(ph, f"hc{i}", [128, NT], F32) for i in range(2)]
        pp = [P.ps(ph, f"plp{i}", [128, 512]) for i in range(2)]
        T.op("sp", "dma_start", C(out=fl[:], in_=G["flags"][:, :]), writes=["fl"], dma=True)
        for gi in range(4):
            T.op("sp", "dma_start", C(out=rc[:, gi, :], in_=G["pool_rc"][gi, :].partition_broadcast(128)), writes=[("rc", gi)], dma=True)
        T.op("sp", "dma_start", C(out=gs[:], in_=G["pool_scale"].rearrange("(c p) -> p c", p=128), allow_slow_non_contiguous=True),
             writes=["gs"], dma=True)
        T.op("dve", "tensor_tensor", C(out=gs[:], in0=gs[:], in1=G["mod"][:, l, 0, 5, :], op=ALU.mult), reads=["gs"], writes=["gs"])
        n_ = 0
        for gi, w in enumerate(POOL_W):
            T.op("pool", "dma_start", C(out=wp[:], in_=fm(G["pool_w"][gi, :, :])), writes=["wp"], dma=True)
            for ci in range(GC):
                kc = gi * GC + ci
                b = n_ % 2
                n_ += 1
                e = ext[b]
                ek = ("ext", b)
                T.op("sp", "dma_start", C(out=e[:, 8:NT + 8], in_=hn1[kc * 128:(kc + 1) * 128, :]), writes=[ek], dma=True)
                T.op("sp", "dma_start", C(out=hl[b][:], in_=ex3[kc * 128:(kc + 1) * 128, :]), writes=[("hl", b)], dma=True)
                T.op("sp", "dma_start", C(out=hr[b][:], in_=ex3[D + kc * 128:D + (kc + 1) * 128, :]), writes=[("hr", b)], dma=True)
                T.op("dve", "tensor_scalar", C(out=e[:, 0:8], in0=hl[b][:, 8:16], scalar1=fl[:, 0:1], scalar2=None, op0=ALU.mult),
                     reads=[("hl", b), "fl"], writes=[ek])
                T.op("dve", "tensor_scalar", C(out=e[:, NT + 8:NT + 16], in0=hr[b][:, 0:8], scalar1=fl[:, 1:2], scalar2=None, op0=ALU.mult),
                     reads=[("hr", b), "fl"], writes=[ek])
                cur, curk, sh = e, ek, 1
                bufs2 = [(sa, "sa"), (sb_, "sbb")]
                bi = 0
                while sh < w:
                    dst, dk = bufs2[bi]
                    bi ^= 1
                    lo = 2 * sh - 1
                    T.op("dve", "tensor_tensor", C(out=dst[:, lo:L], in0=cur[:, lo - sh:L - sh], in1=cur[:, lo:L], op=ALU.add),
                         reads=[curk], writes=[dk])
                    cur, curk, sh = dst, dk, sh * 2
                o = w // 2 - 1 + 8
                dst, dk = bufs2[bi]
                T.op("dve", "tensor_tensor", C(out=dst[:, 0:NT], in0=cur[:, o:o + NT], in1=rc[:, gi, :], op=ALU.mult),
                     reads=[curk, ("rc", gi)], writes=[dk])
                T.op("dve", "tensor_tensor", C(out=pooled[:, ci, :], in0=dst[:, 0:NT], in1=e[:, 8:NT + 8], op=ALU.subtract),
                     reads=[dk, ek], writes=[("pooled", ci)])
            for oc in range(GC):
                kc = gi * GC + oc
                hb_ = (gi * GC + oc) % 2
                T.op("sp", "dma_start", C(out=hc[hb_][:], in_=hT[kc * 128:(kc + 1) * 128, 0:NT]), writes=[("hc", hb_)], dma=True)
                for q0 in range(0, NT, QB):
                    pi = (q0 // QB) % 2
                    for ic in range(GC):
                        T.op("pe", "matmul", C(pp[pi][:, :QB], lhsT=wp[:, ic, oc * 128:(oc + 1) * 128], rhs=pooled[:, ic, q0:q0 + QB],
                                               start=(ic == 0), stop=(ic == GC - 1)),
                             reads=["wp", ("pooled", ic)], writes=[("plp", pi)], signal=(ic == GC - 1))
                    T.op("dve", "scalar_tensor_tensor", C(out=hc[hb_][:, q0:q0 + QB], in0=pp[pi][:, :QB], scalar=gs[:, kc:kc + 1],
                                                          in1=hc[hb_][:, q0:q0 + QB], op0=ALU.mult, op1=ALU.add),
                         reads=[("plp", pi), ("hc", hb_), "gs"], writes=[("hc", hb_)])
                T.op("sp", "dma_start", C(out=hT[kc * 128:(kc + 1) * 128, 0:NT], in_=hc[hb_][:]), reads=[("hc", hb_)],
                     writes=[("hT_d", "pool", kc)], dma=True)
        T.flush()

def build(cfg):
    P = Prog(cfg)
    G = {"out_tokens": []}
    T = P.T
    D, NT, NCX, NTOK = cfg.D, cfg.NT, cfg.NCX, cfg.NTOK
    x_d = P.din("x", [NT, D])
    ctx_d = P.din("ctx", [NCX, D])
    out_d = P.dout("out", [NT, D])
    G["hT_d"] = P.dscratch("hT_d", [D, NTOK])
    for l_ in range(2):
        for j_ in range(2):
            G["ffn_w_gate", l_, j_] = P.din(f"ffn_w_gate_{l_}{j_}", [D, cfg.DFF])
            G["ffn_w_up", l_, j_] = P.din(f"ffn_w_up_{l_}{j_}", [D, cfg.DFF])
            G["ffn_w_down", l_, j_] = P.din(f"ffn_w_down_{l_}{j_}", [cfg.DFF, D])
    stage = cfg.phases
    phase_consts(P, G)
    phase_adaln(P, G)
    phase_load_T(P, G, x_d, G["hT_d"], NT, 0)
    phase_load_T(P, G, ctx_d, G["hT_d"], NCX, NT)
    if stage != "none":
        phase_ffn(P, G, 0, 0, True)
    if stage.startswith("s_"):
        full = False
    if stage in ("mla", "all", "l0", "ssd") or stage.startswith("s_"):
        G["w_in2"] = P.din("w_in2", [D, NPROJ])
        G["w_dt"] = P.din("w_dt", [D, 64])
        G["q_norm"] = P.din("q_norm", [QL])
        G["kv_norm"] = P.din("kv_norm", [KVL])
        G["rope_cc"] = P.din("rope_cc", [64, NT])
        G["rope_ss"] = P.din("rope_ss", [64, NT])
        G["w_ukv"] = P.din("w_ukv", [KVL, MLA_H * 256])
        G["w_uq_n"] = P.din("w_uq_n", [QL, MLA_H * 128])
        G["w_uq_r"] = P.din("w_uq_r", [QL, MLA_H * 64])
        G["w_uq_rs"] = P.din("w_uq_rs", [QL, MLA_H * 64])
        G["w_out"] = P.din("w_out", [MIXW, D])
        G["proj_d"] = P.dscratch("proj_d", [NPROJ, NTOK])
        G["dt_d"] = P.dscratch("dt_d", [NTOK, 64])
        G["qcn_d"] = P.dscratch("qcn_d", [QL, NT])
        G["kvn_ctx_d"] = P.dscratch("kvn_ctx_d", [KVL, NCX])
        G["krc_d"] = P.dscratch("krc_d", [64, NCX])
        G["ex1_in"] = [P.dscratch(f"ex1_in{c}", [128 if c < 4 else 64, NT]) for c in range(5)]
        G["ex1_out"] = [P.dscratch(f"ex1_out{c}", [256 if c < 4 else 128, NT]) for c in range(5)]
        G["exchH_in"] = P.dscratch("exchH_in", [3072, 4])
        G["exchH_out"] = P.dscratch("exchH_out", [6144, 4])
        G["mixT_d"] = P.dscratch("mixT_d", [MIXW, NT], BF16)
        phase_inproj(P, G, 0)
        if stage != "s_inproj":
            phase_kvprep(P, G)
        if stage not in ("s_inproj", "s_kvprep"):
            phase_exchange(P, G, [(G["ex1_in"][c], G["ex1_out"][c]) for c in range(5)] + [(G["exchH_in"], G["exchH_out"])])
        if stage in ("mla", "s_zero", "s_attn", "s_outproj"):
            phase_zero_rows(P, G, G["mixT_d"], SSM_IN, NT, BF16)
        elif stage.startswith("s_"):
            pass
        else:
            for nm, shp in (("conv_w", [5, 3072]), ("conv_b", [3072]), ("dt_bias", [64]), ("a_log", [64]),
                            ("dsk", [128, 16]), ("ssm_norm", [SSM_IN]), ("flags", [128, 4]),
                            ("c_tri_f", [128, 128]), ("c_tri_b", [128, 128]), ("c_negm_f", [128, 512]), ("c_negm_b", [128, 512])):
                G[nm] = P.din(nm, shp)
            G["xc_d"] = P.dscratch("xc_d", [3072, NTOK])
            G["xc_tm_d"] = P.dscratch("xc_tm_d", [NTOK, 2560])
            G["sctx_d"] = P.dscratch("sctx_d", [1024, 512])
            G["ex2_in"] = [P.dscratch(f"ex2_in{d}", [512, 512]) for d in range(2)]
            G["ex2_out"] = [P.dscratch(f"ex2_out{d}", [1024, 512]) for d in range(2)]
            phase_conv(P, G)
            with ExitStack() as stk:
                phase_ssd_prep(P, G, stk)
                phase_ssd0(P, G)
                phase_exchange(P, G, [(G["ex2_in"][d], G["ex2_out"][d]) for d in range(2)])
                phase_ssd1(P, G)
        if stage == "ssd":
            phase_zero_rows(P, G, G["mixT_d"][SSM_IN:MIXW, :], MIXW - SSM_IN, NT, BF16)
        elif stage in ("s_inproj", "s_kvprep", "s_exch", "s_zero"):
            pass
        else:
            phase_attn(P, G)
        if not stage.startswith("s_") or stage == "s_outproj":
            phase_outproj(P, G, 0)
        if not stage.startswith("s_"):
            phase_ffn(P, G, 0, 1, False)
    if stage == "all":
        G["pool_w"] = P.din("pool_w", [4, D // 4, D // 4])
        G["pool_scale"] = P.din("pool_scale", [D])
        G["pool_rc"] = P.din("pool_rc", [4, NT])
        G["hn1_d"] = P.dscratch("hn1_d", [D, NT])
        G["exch3_in"] = P.dscratch("exch3_in", [D, 16])
        G["exch3_out"] = P.dscratch("exch3_out", [2 * D, 16])
        phase_ffn(P, G, 1, 0, False)
        phase_pool_a(P, G, 1)
        phase_exchange(P, G, [(G["exch3_in"], G["exch3_out"])])
        phase_pool_b(P, G, 1)
        phase_ffn(P, G, 1, 1, False)
    if getattr(cfg, "debug", False):
        for nm in ("hT_d",):
            if nm in G:
                src = G[nm]
                dd = P.dout("dbg_" + nm, list(src.shape), src.dtype)
                T.op("sp", "dma_start", C(out=dd[:, :], in_=src[:, :]), writes=["dbg" + nm], dma=True)
        if "tabs" in G:
            pass
        T.flush()
    phase_final(P, G, out_d)
    T.wait_tokens("sp", G["out_tokens"])
    T.emit()
    if getattr(cfg, "simcheck", False):
        P.simreport = T.simulate()
    return P


GRID_W = 64
ROPE_THETA = 10000.0


def host_consts(cfg, s):
    NT = cfg.NT
    pos = np.arange(s * NT, (s + 1) * NT)
    row = (pos // GRID_W).astype(np.float32)
    col = (pos % GRID_W).astype(np.float32)
    inv = (ROPE_THETA ** (-np.arange(16, dtype=np.float32) / 16)).astype(np.float32)
    ang = np.concatenate([row[:, None] * inv, col[:, None] * inv], axis=-1).astype(np.float32)
    cos, sin = np.cos(ang).T.astype(np.float32), np.sin(ang).T.astype(np.float32)
    k = np.arange(128)
    tri_f = (k[:, None] <= k[None, :]).astype(np.float32)
    tri_b = (k[:, None] >= k[None, :]).astype(np.float32)
    negm_f = np.where(k[None, :] >= k[:, None], 0.0, -30000.0).astype(np.float32)
    negm_b = np.where(k[None, :] <= k[:, None], 0.0, -30000.0).astype(np.float32)
    nseq = 2 * NT
    rc = np.empty((4, NT), np.float32)
    for gi, w in enumerate((2, 4, 8, 16)):
        lo = np.clip(pos - w // 2, 0, nseq)
        hi = np.clip(pos + w // 2, 0, nseq)
        rc[gi] = (1.0 / (hi - lo).astype(np.float32)).astype(np.float32)
    return {
        "pool_rc": rc,
        "c_tri_f": tri_f, "c_tri_b": tri_b,
        "c_negm_f": np.ascontiguousarray(np.concatenate([negm_f] * 4, 1)), "c_negm_b": np.ascontiguousarray(np.concatenate([negm_b] * 4, 1)),
        "flags": np.ascontiguousarray(np.broadcast_to(np.array([s, 1 - s, 0, 0], np.float32), (128, 4))),
        "c_ident": np.eye(128, dtype=np.float32),
        "rope_cc": np.ascontiguousarray(np.concatenate([cos, cos], 0)),
        "rope_ss": np.ascontiguousarray(np.concatenate([-sin, sin], 0)),
    }


def host_shared(inp):
    f = lambda a: np.ascontiguousarray(np.asarray(a, dtype=np.float32))
    w_in = np.asarray(inp["w_in"][0])
    o = np.cumsum([0, 512, 512, 64, 2048, 3072, 64])
    q_c, kv_c, k_r, z, xbc, dt = [w_in[:, o[i]:o[i + 1]] for i in range(6)]
    k_rs = np.concatenate([k_r[:, 32:], k_r[:, :32]], 1)
    w_uq = np.asarray(inp["w_uq"][0]).reshape(QL, MLA_H, NOPE + ROPE)
    ffn = {}
    for l_ in range(2):
        for j_ in range(2):
            for nm in ("ffn_w_gate", "ffn_w_up", "ffn_w_down"):
                ffn[f"{nm}_{l_}{j_}"] = f(np.asarray(inp[nm])[l_, j_])
    return {
        **ffn,
        "w_in2": f(np.concatenate([q_c, kv_c, k_r, k_rs, z, xbc], 1)),
        "w_dt": f(dt),
        "q_norm": f(inp["q_norm"][0]), "kv_norm": f(inp["kv_norm"][0]),
        "w_ukv": f(inp["w_ukv"][0]),
        "w_uq_n": f(w_uq[:, :, :NOPE].reshape(QL, -1)),
        "w_uq_r": f(w_uq[:, :, NOPE:].reshape(QL, -1)),
        "w_uq_rs": f(np.concatenate([w_uq[:, :, NOPE + 32:], w_uq[:, :, NOPE:NOPE + 32]], 2).reshape(QL, -1)),
        "w_out": f(inp["w_out"][0]),
        "final_norm": f(inp["final_norm"]),
        "conv_w": f(inp["conv_w"][0]), "conv_b": f(inp["conv_b"][0]),
        "dt_bias": f(np.asarray(inp["dt_bias"][0]).reshape(64)), "a_log": f(np.asarray(inp["a_log"][0]).reshape(64)),
        "dsk": f(np.repeat(np.asarray(inp["d_skip"][0]), 64).reshape(16, 128).T),
        "ssm_norm": f(inp["ssm_norm"][0]),
        "pool_w": f(inp["pool_w"][0]), "pool_scale": f(inp["pool_scale"][0]),
    }


def host_layout(inp, cfg, core, shared):
    b, s = core // 2, core % 2
    NT = cfg.NT
    NMS = 9 * cfg.D // cfg.ncores
    f = lambda a: np.ascontiguousarray(np.asarray(a, dtype=np.float32))
    m = dict(shared)
    m.update({
        "x": f(inp["x"][b, s * NT:(s + 1) * NT]),
        "ctx": f(inp["ctx"][b]),
        "ccT_all": f(np.concatenate([np.asarray(inp["c"]).T, np.asarray(inp["c_ctx"])[:, None],
                                     np.zeros((cfg.D, cfg.NRP - cfg.NB - 1), np.float32)], 1)),
        "sel": f(np.eye(cfg.NRP)[:, [b, cfg.NB]]),
        "mod_w_s": f(np.asarray(inp["mod_w"])[:, :, core * NMS:(core + 1) * NMS]),
        "mod_b_s": f(np.asarray(inp["mod_b"])[:, core * NMS:(core + 1) * NMS]),
    })
    m.update(host_consts(cfg, s))
    return m


def kernel(**inputs):
    x = np.asarray(inputs["x"])
    B, SEQ, D = x.shape
    NCX = inputs["ctx"].shape[1]
    DFF = inputs["ffn_w_gate"].shape[-1]
    NT = SEQ // 2
    cfg = Cfg(D=D, DFF=DFF, NT=NT, NCX=NCX, TB=min(512, NT), ncores=2 * B, NB=B, phases=inputs.pop("_stage", "all"),
              debug=inputs.pop("_debug", False))
    P = build(cfg)
    in_maps = []
    shared = host_shared(inputs)
    for core in range(cfg.ncores):
        m = host_layout(inputs, cfg, core, shared)
        in_maps.append({k: m[k] for k in P.inputs})
    res = run_bass_kernel_spmd(P.nc, in_maps, core_ids=list(range(cfg.ncores)))
    out = np.empty((B, SEQ, D), np.float32)
    for core in range(cfg.ncores):
        b, s = core // 2, core % 2
        out[b, s * NT:(s + 1) * NT] = res.results[core]["out"]
    kernel.last = res
    return out
```
